# Optimizing a Trainium2 kernel written in Bass

```python
import math
import jax, jax.numpy as jnp
from jax import lax
import numpy as np

D_MODEL = 2048
BATCH = 8
SEQ = 2048
DEPTH = 4
DEC_BATCH = 4
DEC_SEQ = 4096
PAST_LEN = 128

GRID_W = 64
HEAD_DIM = 128
N_MIXERS = 2
NA_HEADS = D_MODEL // HEAD_DIM
NA_ROWS = 8
NA_COLS = 16
DIL_CONFIG = ((128, 1), (512, 4), (2048, 16))
N_GROUPS = len(DIL_CONFIG)
DIL_HEADS = D_MODEL // (2 * HEAD_DIM)
N_BUCKETS = 32
MAX_DISTANCE = 1024
D_FF = 5632
N_NA_LAYERS = (DEPTH + 1) // 2
N_DIL_LAYERS = DEPTH // 2
ALPHA = (2 * DEPTH) ** 0.25
BETA = (8 * DEPTH) ** -0.25
LN_EPS = 1e-5
NEG_INF = -1e30

kernel_name = "hybrid_na_dilated_macaron_encoder"


def layer_norm(x, g, b):
    xf = x.astype(jnp.float32)
    mu = jnp.mean(xf, axis=-1, keepdims=True)
    var = jnp.mean(jnp.square(xf - mu), axis=-1, keepdims=True)
    y = (xf - mu) * lax.rsqrt(var + LN_EPS) * g.astype(jnp.float32) + b.astype(jnp.float32)
    return y.astype(x.dtype)


def post_norm(x, sub, g, b):
    return layer_norm(ALPHA * x + sub, g, b)


def swiglu(x, w_gate, w_up, w_down):
    return (jax.nn.silu(x @ w_gate) * (x @ w_up)) @ w_down


def t5_bucket(rel):
    half = N_BUCKETS // 2
    max_exact = half // 2
    sign = jnp.where(rel > 0, half, 0)
    n = jnp.abs(rel)
    nf = jnp.maximum(n, 1).astype(jnp.float32)
    large = max_exact + (jnp.log(nf / max_exact) / math.log(MAX_DISTANCE / max_exact)
                         * (half - max_exact)).astype(jnp.int32)
    large = jnp.minimum(large, half - 1)
    return sign + jnp.where(n < max_exact, n, large)


def neighbourhood_mixer(x, w_qkv, w_o, rpb):
    B, T, _ = x.shape
    rows = T // GRID_W
    kh = min(NA_ROWS, rows)
    qkv = (x @ w_qkv).reshape(B, rows, GRID_W, 3, NA_HEADS, HEAD_DIM)
    q, k, v = qkv[:, :, :, 0], qkv[:, :, :, 1], qkv[:, :, :, 2]
    r = jnp.arange(rows)
    row_idx = jnp.clip(r - kh // 2, 0, rows - kh)[:, None] + jnp.arange(kh)[None, :]
    k_rows = k[:, row_idx]
    v_rows = v[:, row_idx].reshape(B, rows, kh * GRID_W, NA_HEADS, HEAD_DIM)
    c = jnp.arange(GRID_W)
    col_start = jnp.clip(c - NA_COLS // 2, 0, GRID_W - NA_COLS)
    col_ok = (c[None, :] >= col_start[:, None]) & (c[None, :] < col_start[:, None] + NA_COLS)
    dr = row_idx - r[:, None] + (NA_ROWS - 1)
    dc = jnp.clip(c[None, :] - c[:, None], -(NA_COLS - 1), NA_COLS - 1) + (NA_COLS - 1)
    bias = rpb[:, dr[:, None, :, None], dc[None, :, None, :]].astype(jnp.float32)
    s = jnp.einsum('brqhd,brawhd->bhrqaw', q, k_rows).astype(jnp.float32) * HEAD_DIM ** -0.5
    s = jnp.where(col_ok[:, None, :], s + bias[None], NEG_INF)
    s = s.reshape(B, NA_HEADS, rows, GRID_W, kh * GRID_W)
    p = jax.nn.softmax(s, axis=-1).astype(v.dtype)
    o = jnp.einsum('bhrqk,brkhd->brqhd', p, v_rows)
    return o.reshape(B, T, NA_HEADS * HEAD_DIM) @ w_o


def banded_attention(q, k, v, radius, bias):
    n, L, H, dh = q.shape
    qb = radius
    kw = qb + 2 * radius
    nb = -(-L // qb)
    lp = nb * qb
    qblk = jnp.pad(q, ((0, 0), (0, lp - L), (0, 0), (0, 0))).reshape(n, nb, qb, H, dh)
    kv_pad = ((0, 0), (radius, lp - L + radius), (0, 0), (0, 0))
    idx = jnp.arange(nb)[:, None] * qb + jnp.arange(kw)[None, :]
    kblk = jnp.pad(k, kv_pad)[:, idx]
    vblk = jnp.pad(v, kv_pad)[:, idx]
    s = jnp.einsum('nbqhd,nbkhd->nhbqk', qblk, kblk).astype(jnp.float32) * dh ** -0.5
    s = s + bias[None, :, None]
    rel = jnp.arange(kw)[None, :] - radius - jnp.arange(qb)[:, None]
    kpos = idx - radius
    valid = ((kpos >= 0) & (kpos < L))[:, None, :] & (jnp.abs(rel) <= radius)[None]
    s = jnp.where(valid[None, None], s, NEG_INF)
    m = jnp.max(s, axis=-1, keepdims=True)
    p = jnp.exp(s - m)
    den = jnp.sum(p, axis=-1, keepdims=True)
    o = jnp.einsum('nhbqk,nbkhd->nbqhd', (p / den).astype(v.dtype), vblk)
    o = o.reshape(n, lp, H, dh)[:, :L]
    lse = (m + jnp.log(den))[..., 0]
    lse = lse.transpose(0, 2, 3, 1).reshape(n, lp, H)[:, :L]
    return o, lse


def dilated_attention(q, k, v, dil, radius, bias):
    B, T, H, dh = q.shape
    L = T // dil

    def to_sub(t):
        return t.reshape(B, L, dil, H, dh).transpose(0, 2, 1, 3, 4).reshape(B * dil, L, H, dh)

    o, lse = banded_attention(to_sub(q), to_sub(k), to_sub(v), radius, bias)
    o = o.reshape(B, dil, L, H, dh).transpose(0, 2, 1, 3, 4).reshape(B, T, H, dh)
    lse = lse.reshape(B, dil, L, H).transpose(0, 2, 1, 3).reshape(B, T, H)
    return o, lse


def dilated_mixer(x, w_qkv, w_o, rel_bias):
    B, T, _ = x.shape
    qkv = (x @ w_qkv).reshape(B, T, N_GROUPS, 3, DIL_HEADS, HEAD_DIM)
    outs, lses = [], []
    for g, (window, dil) in enumerate(DIL_CONFIG):
        radius = window // (2 * dil)
        rel = jnp.arange(3 * radius)[None, :] - radius - jnp.arange(radius)[:, None]
        bias = rel_bias[t5_bucket(rel * dil)][:, :, g * DIL_HEADS:(g + 1) * DIL_HEADS]
        bias = bias.transpose(2, 0, 1).astype(jnp.float32)
        o, lse = dilated_attention(qkv[:, :, g, 0], qkv[:, :, g, 1], qkv[:, :, g, 2], dil, radius, bias)
        outs.append(o)
        lses.append(lse)
    wts = jax.nn.softmax(jnp.stack(lses, axis=0), axis=0)
    o = jnp.sum(wts[..., None].astype(x.dtype) * jnp.stack(outs, axis=0), axis=0)
    return o.reshape(B, T, DIL_HEADS * HEAD_DIM) @ w_o


def trunk(x, ln_g, ln_b, ffn_w_gate, ffn_w_up, ffn_w_down, na_w_qkv, na_w_o, na_rpb,
          dil_w_qkv, dil_w_o, rel_bias):
    for i in range(DEPTH):
        j = i // N_MIXERS
        x = post_norm(x, 0.5 * swiglu(x, ffn_w_gate[i, 0], ffn_w_up[i, 0], ffn_w_down[i, 0]),
                      ln_g[i, 0], ln_b[i, 0])
        if i % N_MIXERS == 0:
            mix = neighbourhood_mixer(x, na_w_qkv[j], na_w_o[j], na_rpb[j])
        else:
            mix = dilated_mixer(x, dil_w_qkv[j], dil_w_o[j], rel_bias)
        x = post_norm(x, mix, ln_g[i, 1], ln_b[i, 1])
        x = post_norm(x, 0.5 * swiglu(x, ffn_w_gate[i, 1], ffn_w_up[i, 1], ffn_w_down[i, 1]),
                      ln_g[i, 2], ln_b[i, 2])
    return x


def setup_inputs(seed: int = 0) -> dict:
    key = jax.random.key(seed)
    ks = jax.random.split(key, 14)

    def nrm(k, shape, scale):
        return jax.random.normal(k, shape, jnp.float32) * scale

    na_width = NA_HEADS * HEAD_DIM
    dil_width = DIL_HEADS * HEAD_DIM
    return {
        "x_prompt": nrm(ks[0], (BATCH, SEQ, D_MODEL), 1.0),
        "x_sample": nrm(ks[1], (DEC_BATCH, DEC_SEQ, D_MODEL), 1.0),
        "ln_g": 1.0 + nrm(ks[2], (DEPTH, 3, D_MODEL), 0.02),
        "ln_b": nrm(ks[3], (DEPTH, 3, D_MODEL), 0.02),
        "ffn_w_gate": nrm(ks[4], (DEPTH, 2, D_MODEL, D_FF), D_MODEL ** -0.5),
        "ffn_w_up": nrm(ks[5], (DEPTH, 2, D_MODEL, D_FF), D_MODEL ** -0.5),
        "ffn_w_down": nrm(ks[6], (DEPTH, 2, D_FF, D_MODEL), BETA * D_FF ** -0.5),
        "na_w_qkv": nrm(ks[7], (N_NA_LAYERS, D_MODEL, 3 * na_width), D_MODEL ** -0.5),
        "na_w_o": nrm(ks[8], (N_NA_LAYERS, na_width, D_MODEL), BETA * na_width ** -0.5),
        "na_rpb": nrm(ks[9], (N_NA_LAYERS, NA_HEADS, 2 * NA_ROWS - 1, 2 * NA_COLS - 1), 0.2),
        "dil_w_qkv": nrm(ks[10], (N_DIL_LAYERS, D_MODEL, N_GROUPS * 3 * dil_width), D_MODEL ** -0.5),
        "dil_w_o": nrm(ks[11], (N_DIL_LAYERS, dil_width, D_MODEL), BETA * dil_width ** -0.5),
        "rel_bias": nrm(ks[12], (N_BUCKETS, N_GROUPS * DIL_HEADS), 0.2),
    }


def reference(x_prompt, x_sample, ln_g, ln_b, ffn_w_gate, ffn_w_up, ffn_w_down, na_w_qkv, na_w_o,
              na_rpb, dil_w_qkv, dil_w_o, rel_bias):
    y_prompt = trunk(x_prompt, ln_g, ln_b, ffn_w_gate, ffn_w_up, ffn_w_down, na_w_qkv, na_w_o,
                     na_rpb, dil_w_qkv, dil_w_o, rel_bias)
    y_sample = trunk(x_sample, ln_g, ln_b, ffn_w_gate, ffn_w_up, ffn_w_down, na_w_qkv, na_w_o,
                     na_rpb, dil_w_qkv, dil_w_o, rel_bias)
    return (y_prompt, y_sample)
```

```python
import math
from contextlib import ExitStack

import numpy as np
import concourse.bass as bass
import concourse.mybir as mybir
from concourse.bass_utils import run_bass_kernel_spmd

F32 = mybir.dt.float32
BF16 = mybir.dt.bfloat16
AF = mybir.ActivationFunctionType
ALU = mybir.AluOpType

P = 128
NEG = -1.0e30
DIL = ((128, 1), (512, 4), (2048, 16))
LN_EPS = 1e-5


class Cfg:
    def __init__(self, D=2048, FF=5632, DEPTH=4, TOK=4096):
        self.D, self.FF, self.DEPTH, self.TOK = D, FF, DEPTH, TOK
        self.KC = D // P
        self.FC = FF // P
        self.NAH = D // P
        self.DILH = D // (2 * P)
        self.DW = D // 2
        self.NNA = (DEPTH + 1) // 2
        self.NDIL = DEPTH // 2
        self.TT = 512
        self.NT = TOK // self.TT
        self.NS = self.TT // P
        self.DB = min(512, D)
        self.NDB = D // self.DB
        self.ALPHA = (2 * DEPTH) ** 0.25
        self.PF = 4 if self.FC % 4 == 0 else self.FC
        self.NPI = self.FC // self.PF


def na_start(r, sample):
    if sample:
        return min(max(r - 4, 0), 56)
    base = (r // 32) * 32
    return base + min(max(r % 32 - 4, 0), 24)


def na_valid(qr, kr, sample):
    s = na_start(qr, sample)
    return s <= kr < s + 8


def na_tiles():
    tiles = []
    for i in range(32):
        ks = set()
        for sample in (False, True):
            for qr in (2 * i, 2 * i + 1):
                s = na_start(qr, sample)
                for kr in range(s, s + 8):
                    ks.add(kr // 2)
        lo, hi = min(ks), max(ks)
        assert hi - lo + 1 <= 6 and abs(lo - i) <= 3 and abs(hi - i) <= 3
        tiles.append(list(range(lo, hi + 1)))
    return tiles


NA_TILES = na_tiles()


def na_full_valid(i, k):
    j = NA_TILES[i][k]
    for sample in (False, True):
        for qrl in (0, 1):
            for krl in (0, 1):
                if not na_valid(2 * i + qrl, 2 * j + krl, sample):
                    return False
    return True


def make_nam(sample):
    m = np.zeros((P, 32 * 6 * 2), np.float32)
    for i in range(32):
        for k, j in enumerate(NA_TILES[i]):
            for qrl in (0, 1):
                for krl in (0, 1):
                    if not na_valid(2 * i + qrl, 2 * j + krl, sample):
                        m[krl * 64:(krl + 1) * 64, (i * 6 + k) * 2 + qrl] = NEG
    return m


def make_em(sample):
    e = np.zeros((P, 8), np.float32)
    e[:64, 1] = NEG
    e[64:, 2] = NEG
    if not sample:
        e[:64, 3] = NEG
        e[64:, 4] = NEG
    return e


def make_nab(rpb):
    L, H = rpb.shape[:2]
    krl = np.arange(2)[:, None, None, None, None]
    kc = np.arange(64)[None, :, None, None, None]
    dl = np.arange(7)[None, None, :, None, None] - 3
    qrl = np.arange(2)[None, None, None, :, None]
    qc = np.arange(64)[None, None, None, None, :]
    dr = 2 * dl + krl - qrl
    dr_ok = np.abs(dr) <= 7
    dri = np.clip(dr + 7, 0, 14)
    dc = np.clip(kc - qc, -15, 15) + 15
    cs = np.clip(qc - 8, 0, 48)
    ok = (kc >= cs) & (kc < cs + 16) & dr_ok
    dri, dc, ok = np.broadcast_arrays(dri, dc, ok)
    g = rpb[:, :, dri, dc]
    out = np.where(ok[None, None], g, np.float32(NEG)).astype(np.float32)
    return np.ascontiguousarray(out.reshape(L, H, P, 7, P))


def t5_bucket_np(rel):
    half, max_exact = 16, 8
    sign = np.where(rel > 0, half, 0)
    n = np.abs(rel)
    nf = np.maximum(n, 1).astype(np.float32)
    large = max_exact + (np.log(nf / np.float32(max_exact)) / np.float32(math.log(1024 / max_exact))
                         * np.float32(half - max_exact)).astype(np.int32)
    large = np.minimum(large, half - 1)
    return sign + np.where(n < max_exact, n, large)


def make_dbt(rel_bias, DILH):
    out = np.full((3 * DILH, P, 2, P), NEG, np.float32)
    p = np.arange(P)[:, None]
    q = np.arange(P)[None, :]
    for g, (_, d) in enumerate(DIL):
        for ab in (0, 1):
            rel = p + (-64 if ab == 0 else 64) - q
            ok = np.abs(rel) <= 64
            bk = t5_bucket_np(rel * d)
            for h in range(DILH):
                vals = rel_bias[bk, g * DILH + h]
                out[g * DILH + h, :, ab, :] = np.where(ok, vals, np.float32(NEG))
    return out


class Rec:
    __slots__ = ("eng", "fn", "deps", "sig", "val", "key", "isdma")


class Ring:
    def __init__(self, n):
        self.n = n
        self.i = 0
        self.readers = [[] for _ in range(n)]

    def next(self):
        s = self.i % self.n
        self.i += 1
        deps = self.readers[s]
        self.readers[s] = []
        return s, deps

    def read(self, s, rec):
        self.readers[s].append(rec)


class Prog:
    ENGS = ("pe", "act", "dve", "pool", "sp")

    def __init__(self):
        self.q = {e: [] for e in self.ENGS}
        self.last_dma = {}
        self.deferred = []

    def op(self, eng, fn, deps=()):
        r = Rec()
        r.eng, r.fn, r.sig, r.val, r.key, r.isdma = eng, fn, False, None, None, False
        r.deps = [d for d in deps if d is not None]
        self.q[eng].append(r)
        return r

    def dma(self, fn, key, deps=(), eng="sp"):
        r = self.op(eng, fn, deps)
        r.isdma, r.key, r.sig = True, key, True
        self.last_dma[key] = r
        return r

    def defer(self, fn, key, deps=()):
        self.deferred.append((fn, key, list(deps)))

    def flush(self):
        for fn, key, deps in self.deferred:
            self.dma(fn, key, deps)
        self.deferred = []

    def barrier(self):
        self.flush()
        lasts = []
        for e in self.ENGS:
            for r in reversed(self.q[e]):
                if r.fn is not None and not r.isdma:
                    lasts.append(r)
                    break
        lasts += list(self.last_dma.values())
        self.last_dma = {}
        for e in self.ENGS:
            self.op(e, None, deps=[l for l in lasts if not (l.eng == e and not l.isdma)])

    def finalize(self):
        for e in self.ENGS:
            for r in self.q[e]:
                for d in r.deps:
                    assert d.fn is not None
                    d.sig = True
        cnt = {}
        for e in self.ENGS:
            for r in self.q[e]:
                if r.sig:
                    if not r.isdma:
                        r.key = ("eng", e)
                    cnt[r.key] = cnt.get(r.key, 0) + (16 if r.isdma else 1)
                    r.val = cnt[r.key]
        self.keys = list(cnt.keys())
        self.maxvals = cnt

    def emit(self, nc):
        hmap = {"pe": nc.tensor, "act": nc.scalar, "dve": nc.vector, "pool": nc.gpsimd, "sp": nc.sync}
        bmap = {"pe": "tensor", "act": "scalar", "dve": "vector", "pool": "gpsimd", "sp": "sync"}
        with ExitStack() as st:
            sems = {}
            for i, k in enumerate(self.keys):
                sems[k] = st.enter_context(nc.semaphore("s%d" % i))
            block = st.enter_context(nc.Block())
            for e in self.ENGS:
                def body(engine, e=e):
                    waited = {}
                    for r in self.q[e]:
                        for d in r.deps:
                            if waited.get(d.key, 0) < d.val:
                                engine.wait_ge(sems[d.key], d.val)
                                waited[d.key] = d.val
                        if r.fn is None:
                            continue
                        ins = r.fn(engine)
                        if r.sig:
                            ins.then_inc(sems[r.key], 16 if r.isdma else 1)
                getattr(block, bmap[e])(body)


def build_program(cfg):
    D, FF, DEPTH, TOK = cfg.D, cfg.FF, cfg.DEPTH, cfg.TOK
    KC, FC, NAH, DILH, DW = cfg.KC, cfg.FC, cfg.NAH, cfg.DILH, cfg.DW
    TT, NT, NS, DB, NDB = cfg.TT, cfg.NT, cfg.NS, cfg.DB, cfg.NDB
    PF, NPI = cfg.PF, cfg.NPI
    ALPHA = cfg.ALPHA
    NBLK = TOK // P
    SCALE = float(P) ** -0.5

    nc = bass.Bass("TRN2", target_bir_lowering=False)

    def dram(name, shape, dt, kind):
        return nc.dram_tensor(name, list(shape), dt, kind=kind).ap()

    x_in = dram("x", [TOK, D], F32, "ExternalInput")
    y = dram("y", [TOK, D], F32, "ExternalOutput")
    wshapes = {
        "ffn_w_gate": [DEPTH, 2, D, FF], "ffn_w_up": [DEPTH, 2, D, FF], "ffn_w_down": [DEPTH, 2, FF, D],
        "na_w_qkv": [cfg.NNA, D, 3 * D], "na_w_o": [cfg.NNA, D, D],
        "dil_w_qkv": [max(cfg.NDIL, 1), D, 9 * DW], "dil_w_o": [max(cfg.NDIL, 1), DW, D],
    }
    w32 = {k: dram(k, s, F32, "ExternalInput") for k, s in wshapes.items()}
    wbf = {k: dram(k + "_bf", s, BF16, "Internal") for k, s in wshapes.items()}
    lng = dram("ln_g", [DEPTH * 3, D], F32, "ExternalInput")
    lnb = dram("ln_b", [DEPTH * 3, D], F32, "ExternalInput")
    nab = dram("nab", [cfg.NNA, NAH, P, 7 * P], F32, "ExternalInput")
    nam = dram("nam", [P, 32 * 6 * 2], F32, "ExternalInput")
    dbt = dram("dbt", [3 * DILH, P, 2 * P], F32, "ExternalInput")
    em = dram("em", [P, 8], F32, "ExternalInput")
    cst = dram("cst", [P, 2 * P], F32, "ExternalInput")
    NQK = max(NAH, 3 * DILH)
    VW = max(D, 3 * DW)
    QT = dram("qt_s", [NQK, P, TOK], BF16, "Internal")
    KTd = dram("kt_s", [NQK, P, TOK], BF16, "Internal")
    Vd = dram("v_s", [TOK, VW], BF16, "Internal")
    OT = dram("ot_s", [D, TOK], BF16, "Internal")

    ARENA_BYTES = 204 * 1024
    st = ExitStack()
    arena = st.enter_context(nc.sbuf_tensor("arena", [P, ARENA_BYTES // 2], BF16))
    ps = st.enter_context(nc.psum_tensor("ps", [P, 8 * 512], F32))

    class Arena:
        def __init__(self):
            self.off = 0
            self.base = 0

        def alloc(self, nbytes):
            o = self.off
            self.off += (nbytes + 63) // 64 * 64
            assert self.off <= ARENA_BYTES, ("SBUF arena overflow", self.off)
            return o

        def view(self, dt, shape):
            n = 1
            for s_ in shape:
                n *= s_
            esz = 4 if dt == F32 else 2
            o = self.alloc(n * esz)
            a = arena[:, o // 2: o // 2 + n * esz // 2]
            if dt == F32:
                a = a.bitcast(F32)
            if len(shape) == 2:
                a = a.rearrange("p (a b) -> p a b", a=shape[0])
            elif len(shape) == 3:
                a = a.rearrange("p (a b c) -> p a b c", a=shape[0], b=shape[1])
            return a

        def mark(self):
            self.base = self.off

        def reset(self):
            self.off = self.base

    A = Arena()
    Pg = Prog()
    banks = Ring(8)

    def bank(b):
        return ps[:, b * 512:(b + 1) * 512]

    cst32 = A.view(F32, [2 * P])
    ident = A.view(BF16, [P])
    ones = A.view(BF16, [P])
    em_sb = A.view(F32, [8])
    A.mark()
    r0 = Pg.dma(lambda e: e.dma_start(out=cst32, in_=cst), "c0")
    r1 = Pg.dma(lambda e: e.dma_start(out=em_sb, in_=em), "c0")
    Pg.op("dve", lambda e: e.tensor_copy(out=ident, in_=cst32[:, 0:P]), deps=[r1])
    Pg.op("dve", lambda e: e.tensor_copy(out=ones, in_=cst32[:, P:2 * P]))

    CH = 4096
    CVB = ARENA_BYTES - 48 * 1024

    def fixed(off, dt, n):
        esz = 4 if dt == F32 else 2
        a_ = arena[:, off // 2: off // 2 + n * esz // 2]
        return a_.bitcast(F32) if dt == F32 else a_

    cvf = [fixed(CVB + i_ * CH * 4, F32, CH) for i_ in range(2)]
    cvb = [fixed(CVB + 2 * CH * 4 + i_ * CH * 2, BF16, CH) for i_ in range(2)]

    def sub_weights(li, part):
        j = li // 2
        if part == "ffn1":
            return [(nm, (li, 0)) for nm in ("ffn_w_gate", "ffn_w_up", "ffn_w_down")]
        if part == "ffn2":
            return [(nm, (li, 1)) for nm in ("ffn_w_gate", "ffn_w_up", "ffn_w_down")]
        if li % 2 == 0:
            return [("na_w_qkv", (j,)), ("na_w_o", (j,))]
        return [("dil_w_qkv", (j,)), ("dil_w_o", (j,))]

    def chunks_of(order, CH=CH):
        out = []
        for nm, idx in order:
            src, dst = w32[nm], wbf[nm]
            for i_ in idx:
                src, dst = src[i_], dst[i_]
            src = src.rearrange("k f -> (k f)").rearrange("(p x) -> p x", p=P)
            dst = dst.rearrange("k f -> (k f)").rearrange("(p x) -> p x", p=P)
            X = src.shape[1]
            for c0 in range(0, X, CH):
                out.append((src, dst, c0, min(CH, X - c0)))
        return out

    def convert_prologue(order):
        NSL = 4
        f32s = [A.view(F32, [CH]) for _ in range(NSL)]
        b16s = [A.view(BF16, [CH]) for _ in range(NSL)]
        lring, sring = Ring(NSL), Ring(NSL)
        for c, (src, dst, c0, n) in enumerate(chunks_of(order)):
            s, ldeps = lring.next()
            ld = Pg.dma(lambda e, s=s, n=n, src=src, c0=c0: e.dma_start(out=f32s[s][:, :n], in_=src[:, c0:c0 + n]),
                        ("cvl", s), deps=ldeps)
            s2, sdeps = sring.next()
            eng = ("dve", "pool", "dve")[c % 3]
            cast = Pg.op(eng, lambda e, s=s, s2=s2, n=n: e.tensor_copy(out=b16s[s2][:, :n], in_=f32s[s][:, :n]),
                         deps=[ld] + sdeps)
            lring.read(s, cast)
            stt = Pg.dma(lambda e, s2=s2, n=n, dst=dst, c0=c0: e.dma_start(out=dst[:, c0:c0 + n], in_=b16s[s2][:, :n]),
                         ("cvs", s2), deps=[cast], eng="act")
            sring.read(s2, stt)

    class Pump:
        def __init__(self, order, cvf, cvb, ch, engs):
            self.chunks = chunks_of(order, ch)
            self.cvf, self.cvb = cvf, cvb
            self.i = 0
            self.loads = {}
            self.lring, self.sring = Ring(2), Ring(2)
            self.engs = engs

        def _load(self, idx):
            src, dst, c0, n = self.chunks[idx]
            s, deps = self.lring.next()
            rec = Pg.dma(lambda e, s=s, n=n, src=src, c0=c0: e.dma_start(out=self.cvf[s][:, :n], in_=src[:, c0:c0 + n]),
                         ("pcl", s), deps=deps, eng="pool")
            self.loads[idx] = (s, rec)

        def pump(self, k):
            for _ in range(k):
                if self.i >= len(self.chunks):
                    return
                if self.i not in self.loads:
                    self._load(self.i)
                if self.i + 1 < len(self.chunks) and (self.i + 1) not in self.loads:
                    self._load(self.i + 1)
                s, ld = self.loads.pop(self.i)
                src, dst, c0, n = self.chunks[self.i]
                s2, sdeps = self.sring.next()
                eng = self.engs[self.i % len(self.engs)]
                if eng == "act":
                    cast = Pg.op("act", lambda e, s=s, s2=s2, n=n: e.copy(out=self.cvb[s2][:, :n], in_=self.cvf[s][:, :n]), deps=[ld] + sdeps)
                else:
                    cast = Pg.op(eng, lambda e, s=s, s2=s2, n=n: e.tensor_copy(out=self.cvb[s2][:, :n], in_=self.cvf[s][:, :n]), deps=[ld] + sdeps)
                self.lring.read(s, cast)
                stt = Pg.dma(lambda e, s2=s2, n=n, dst=dst, c0=c0: e.dma_start(out=dst[:, c0:c0 + n], in_=self.cvb[s2][:, :n]),
                             ("pcs", s2), deps=[cast], eng="pool")
                self.sring.read(s2, stt)
                self.i += 1

        def upto(self, frac):
            tgt = int(math.ceil(frac * len(self.chunks)))
            if tgt > self.i:
                self.pump(tgt - self.i)

        def drain(self):
            self.pump(len(self.chunks) - self.i)

    pumps = {"cur": None}

    def pump_to(frac, engs=None):
        pm = pumps["cur"]
        if pm is not None:
            pm.upto(frac)

    convert_prologue(sub_weights(0, "ffn1"))
    Pg.barrier()
    A.reset()

    state = {"first": True}

    def xsrc():
        return x_in if state["first"] else y

    def transposes(xb, s, XT, xt_wdeps, cast_rec, evac_i):
        evs = []
        per = 8
        for k0 in range(0, KC, per):
            nk = min(per, KC - k0)
            b, bdeps = banks.next()
            pb = bank(b).bitcast(BF16)
            last = None
            for kk in range(nk):
                kc = k0 + kk
                last = Pg.op("pe", lambda e, pb=pb, kk=kk, kc=kc, xb=xb: e.transpose(
                    pb[:, kk * P:(kk + 1) * P], xb[:, kc * P:(kc + 1) * P], ident),
                    deps=([cast_rec] + bdeps) if kk == 0 else ())
            eng = ("act", "dve")[evac_i[0] % 2]
            evac_i[0] += 1
            src = pb[:, 0:nk * P].rearrange("p (a b) -> p a b", a=nk)
            dstv = XT[:, k0:k0 + nk, s * P:(s + 1) * P]
            if eng == "act":
                ev = Pg.op("act", lambda e, src=src, dstv=dstv: e.copy(out=dstv, in_=src), deps=[last] + xt_wdeps)
            else:
                ev = Pg.op("dve", lambda e, src=src, dstv=dstv: e.tensor_copy(out=dstv, in_=src), deps=[last] + xt_wdeps)
            banks.read(b, ev)
            evs.append(ev)
        return evs

    def ln_sub(t, s, YV, ready, XO, xoring, LNG, LNB, ST, MV, SD, RS, smring):
        nch = (D + 511) // 512
        cw = D // nch
        sm, smdeps = smring.next()
        stv = ST[sm]
        last = None
        for c in range(nch):
            last = Pg.op("dve", lambda e, c=c: e.bn_stats(out=stv[:, c * 6:(c + 1) * 6], in_=YV[:, c * cw:(c + 1) * cw]),
                         deps=(ready + smdeps) if c == 0 else ())
        ag = Pg.op("dve", lambda e: e.bn_aggr(out=MV[sm], in_=stv[:, 0:nch * 6]), deps=[last])
        sq = Pg.op("act", lambda e: e.activation(out=SD[sm], in_=MV[sm][:, 1:2], func=AF.Sqrt,
                                                 bias=eps_sb[:, 0:1], scale=1.0), deps=[ag])
        rc = Pg.op("dve", lambda e: e.reciprocal(out=RS[sm], in_=SD[sm]), deps=[sq])
        xo, xodeps = xoring.next()
        p1 = Pg.op("dve", lambda e: e.scalar_tensor_tensor(out=XO[xo], in0=YV, scalar=MV[sm][:, 0:1], in1=LNG,
                                                           op0=ALU.subtract, op1=ALU.mult), deps=[ag] + xodeps)
        p2 = Pg.op("dve", lambda e: e.scalar_tensor_tensor(out=XO[xo], in0=XO[xo], scalar=RS[sm][:, 0:1], in1=LNB,
                                                           op0=ALU.mult, op1=ALU.add), deps=[rc, p1])
        smring.read(sm, p2)
        r0_ = t * TT + s * P
        stt = Pg.dma(lambda e: e.dma_start(out=y[r0_:r0_ + P, :], in_=XO[xo]), "st_x", deps=[p2], eng="act")
        xoring.read(xo, stt)
        return p1

    eps_sb = A.view(F32, [1])
    A.mark()
    Pg.op("dve", lambda e: e.memset(eps_sb, LN_EPS))

    def load_ln(idx, LNG, LNB, deps):
        a = Pg.dma(lambda e: e.dma_start(out=LNG, in_=lng[idx:idx + 1, :].partition_broadcast(P)[:, 0, :]), "ln", deps=deps)
        b = Pg.dma(lambda e: e.dma_start(out=LNB, in_=lnb[idx:idx + 1, :].partition_broadcast(P)[:, 0, :]), "ln", deps=deps)
        return b

    def ffn_phase(li, which):
        A.reset()
        XT = A.view(BF16, [KC, TT])
        HT = A.view(BF16, [FC, TT])
        NWG = 4
        WGU = [A.view(BF16, [KC, 2 * P if FC >= 2 else P]) for _ in range(NWG)]
        NWD = 6
        WD = [A.view(BF16, [PF, DB]) for _ in range(NWD)]
        YACC = A.view(F32, [NS, D])
        XB = [A.view(BF16, [D]) for _ in range(2)]
        XO = [A.view(F32, [D]) for _ in range(1)]
        LNG = A.view(F32, [D])
        LNB = A.view(F32, [D])
        SG = [A.view(F32, [TT]) for _ in range(2)]
        ST = [A.view(F32, [24]) for _ in range(2)]
        MV = [A.view(F32, [2]) for _ in range(2)]
        SD = [A.view(F32, [1]) for _ in range(2)]
        RS = [A.view(F32, [1]) for _ in range(2)]
        wgr, wdr, xbr, xor_, sgr, smr = Ring(NWG), Ring(NWD), Ring(2), Ring(1), Ring(2), Ring(2)
        CHF = 1024
        fcvf = [A.view(F32, [CHF]) for _ in range(2)]
        fcvb = [A.view(BF16, [CHF]) for _ in range(2)]
        if which == 0:
            corder = sub_weights(li, "mix") + sub_weights(li, "ffn2")
        else:
            corder = sub_weights(li + 1, "ffn1") if li + 1 < DEPTH else []
        pm = Pump(corder, fcvf, fcvb, CHF, ("dve",))
        lnrec = load_ln(li * 3 + (0 if which == 0 else 2), LNG, LNB, [])
        G = 2 if FC >= 2 else 1
        GW = G * P
        wg = wbf["ffn_w_gate"][li, which].rearrange("(kc p) f -> p kc f", p=P)
        wu = wbf["ffn_w_up"][li, which].rearrange("(kc p) f -> p kc f", p=P)
        wd = wbf["ffn_w_down"][li, which].rearrange("(fc p) d -> p fc d", p=P)
        src = xsrc()
        yacc_readers = []
        xt_readers = []
        ht_readers = []
        evac_i = [0]
        hold = {"xt_readers": []}

        def front_sub(t_, s):
            xo, xodeps = xor_.next()
            r0_ = t_ * TT + s * P
            ld = Pg.dma(lambda e: e.dma_start(out=XO[xo], in_=src[r0_:r0_ + P, :]), "xstg", deps=xodeps)
            xs, xdeps = xbr.next()
            cast = Pg.op("dve" if s % 2 == 0 else "pool", lambda e: e.tensor_copy(out=XB[xs], in_=XO[xo]),
                         deps=[ld] + xdeps)
            xor_.read(xo, cast)
            evs = transposes(XB[xs], s, XT, hold["xt_readers"], cast, evac_i)
            xbr.read(xs, evs[-1])
            return evs

        xt_evs = []
        for s in range(NS):
            xt_evs += front_sub(0, s)
        ngroups = FC // (2 if FC >= 2 else 1)
        yl_at = min(NS - 1, ngroups - 1)
        yl_at2 = min(NS + 5, ngroups - 1)
        pend = None
        scl = []
        for t in range(NT):
            t0 = t * TT
            ln_readers = []
            first_mm = True
            for g in range(FC // G):
                f0 = g * GW
                sg_, gdeps = wgr.next()
                lg = Pg.dma(lambda e, sg_=sg_, f0=f0: e.dma_start(out=WGU[sg_][:, :, 0:GW], in_=wg[:, :, f0:f0 + GW]),
                            ("wgu", sg_), deps=gdeps)
                su_, udeps = wgr.next()
                lu = Pg.dma(lambda e, su_=su_, f0=f0: e.dma_start(out=WGU[su_][:, :, 0:GW], in_=wu[:, :, f0:f0 + GW]),
                            ("wgu", su_), deps=udeps)
                for fl in range(G):
                    fc = g * G + fl
                    bg, bgd = banks.next()
                    lastg = None
                    for kc in range(KC):
                        d_ = []
                        if kc == 0:
                            d_ = bgd + ([lg] if fl == 0 else []) + (xt_evs if first_mm else [])
                        lastg = Pg.op("pe", lambda e, bg=bg, sg_=sg_, kc=kc, fl=fl: e.matmul(
                            bank(bg)[:, 0:TT], WGU[sg_][:, kc, fl * P:(fl + 1) * P], XT[:, kc, :],
                            start=(kc == 0), stop=(kc == KC - 1)), deps=d_)
                    first_mm = False
                    bu, bud = banks.next()
                    lastu = None
                    for kc in range(KC):
                        d_ = (bud + ([lu] if fl == 0 else [])) if kc == 0 else []
                        lastu = Pg.op("pe", lambda e, bu=bu, su_=su_, kc=kc, fl=fl: e.matmul(
                            bank(bu)[:, 0:TT], WGU[su_][:, kc, fl * P:(fl + 1) * P], XT[:, kc, :],
                            start=(kc == 0), stop=(kc == KC - 1)), deps=d_)
                    if fl == G - 1:
                        wgr.read(sg_, lastg)
                        wgr.read(su_, lastu)
                    ss, sdeps = sgr.next()
                    sl = Pg.op("act", lambda e, ss=ss, bg=bg: e.activation(out=SG[ss], in_=bank(bg)[:, 0:TT], func=AF.Silu),
                               deps=[lastg] + sdeps)
                    banks.read(bg, sl)
                    mu = Pg.op("dve", lambda e, ss=ss, bu=bu, fc=fc: e.tensor_tensor(
                        out=HT[:, fc, :], in0=SG[ss], in1=bank(bu)[:, 0:TT], op=ALU.mult),
                        deps=[sl, lastu] + (ht_readers if (g == 0 and fl == 0) else []))
                    banks.read(bu, mu)
                    sgr.read(ss, mu)
                    ht_last = mu
                    xt_last = lastu
                if pend is not None and g < NS:
                    ln_readers.append(ln_sub(pend[0], g, YACC[:, g, :], pend[1][g], XO, xor_, LNG, LNB, ST, MV, SD, RS, smr))
                if g == yl_at and pend is not None:
                    for s2 in range(g + 1, NS):
                        ln_readers.append(ln_sub(pend[0], s2, YACC[:, s2, :], pend[1][s2], XO, xor_, LNG, LNB, ST, MV, SD, RS, smr))
                pm.upto((t * ngroups + g + 1) / float(NT * ngroups))
                if g == yl_at2:
                    lds = []
                    for s2 in range(NS):
                        lds.append(Pg.dma(lambda e, s2=s2, t0=t0: e.dma_start(out=YACC[:, s2, :], in_=src[t0 + s2 * P:t0 + (s2 + 1) * P, :]),
                                          "yacc", deps=ln_readers if s2 == 0 else ()))
                    scl = []
                    for s2 in range(NS):
                        scl.append(Pg.op("act", lambda e, s2=s2: e.mul(out=YACC[:, s2, :], in_=YACC[:, s2, :], mul=ALPHA), deps=[lds[-1]]))
            hold["xt_readers"] = [xt_last]
            ht_readers = []
            next_evs = []
            yacc_ready = [[] for _ in range(NS)]
            for db in range(NDB):
                if t + 1 < NT:
                    for s in range(NS):
                        if s * NDB // NS == db:
                            next_evs += front_sub(t + 1, s)
                bs = []
                for s in range(NS):
                    bs.append(banks.next())
                lastmm = [None] * NS
                for pi in range(NPI):
                    sw, wdeps = wdr.next()
                    lw = Pg.dma(lambda e, sw=sw, pi=pi, db=db: e.dma_start(
                        out=WD[sw], in_=wd[:, pi * PF:(pi + 1) * PF, db * DB:(db + 1) * DB]), ("wd", sw), deps=wdeps)
                    for s in range(NS):
                        b, bdeps = bs[s]
                        for j_ in range(PF):
                            fc = pi * PF + j_
                            d_ = []
                            if j_ == 0:
                                d_ = [lw]
                                if pi == 0:
                                    d_ = d_ + bdeps + ([ht_last] if (db == 0 and s == 0) else [])
                            lastmm[s] = Pg.op("pe", lambda e, b=b, fc=fc, s=s, sw=sw, j_=j_: e.matmul(
                                bank(b)[:, 0:DB], HT[:, fc, s * P:(s + 1) * P], WD[sw][:, j_, :],
                                start=(fc == 0), stop=(fc == FC - 1)), deps=d_)
                    wdr.read(sw, lastmm[NS - 1])
                for s in range(NS):
                    b = bs[s][0]
                    ev = Pg.op("dve", lambda e, b=b, s=s, db=db: e.scalar_tensor_tensor(
                        out=YACC[:, s, db * DB:(db + 1) * DB], in0=bank(b)[:, 0:DB], scalar=0.5,
                        in1=YACC[:, s, db * DB:(db + 1) * DB], op0=ALU.mult, op1=ALU.add),
                        deps=[lastmm[s], scl[s]])
                    banks.read(b, ev)
                    yacc_ready[s] = [ev]
            ht_readers = [lastmm[NS - 1]]
            for s in range(NS):
                yacc_ready[s] = yacc_ready[s] + [lnrec]
            pend = (t, yacc_ready)
            xt_evs = next_evs
        for s in range(NS):
            ln_sub(pend[0], s, YACC[:, s, :], pend[1][s], XO, xor_, LNG, LNB, ST, MV, SD, RS, smr)
        pm.drain()
        state["first"] = False
        Pg.barrier()

    def qkv_phase(li, f0=0.0, f1=0.5):
        A.reset()
        mt, j = li % 2, li // 2
        XT = A.view(BF16, [KC, TT])
        XF = [A.view(F32, [D]) for _ in range(2)]
        XB = [A.view(BF16, [D]) for _ in range(2)]
        HG = min(2, NAH if mt == 0 else DILH)
        NWQ = 4
        WQ = [A.view(BF16, [KC, HG * P]) for _ in range(NWQ)]
        if mt == 0:
            w = wbf["na_w_qkv"][j].rearrange("(kc p) f -> p kc f", p=P)
            qk = [("q", h, h * P) for h in range(NAH)] + [("k", h, D + h * P) for h in range(NAH)]
            VBW = min(512, D)
            vblocks = [(2 * D + c0, c0, VBW) for c0 in range(0, D, VBW)]
        else:
            w = wbf["dil_w_qkv"][j].rearrange("(kc p) f -> p kc f", p=P)
            qk = []
            for g in range(3):
                for c_, nm in ((0, "q"), (1, "k")):
                    for h in range(DILH):
                        qk.append((nm, g * DILH + h, ((g * 3 + c_) * DILH + h) * P))
            VBW = min(512, DW)
            vblocks = []
            for g in range(3):
                for c0 in range(0, DW, VBW):
                    vblocks.append(((g * 3 + 2) * DW + c0, g * DW + c0, VBW))
        NWV = 2
        WV = [A.view(BF16, [KC, VBW]) for _ in range(NWV)]
        QS = [A.view(BF16, [TT]) for _ in range(4)]
        VS = [A.view(BF16, [VBW]) for _ in range(4)]
        xfr, xbr, wqr, wvr, qsr, vsr = Ring(2), Ring(2), Ring(NWQ), Ring(NWV), Ring(4), Ring(4)
        assert A.off <= CVB, A.off
        npc = (len(qk) + HG - 1) // HG + len(vblocks)
        pc = [0]
        xt_readers = []
        evac_i = [0]
        src = xsrc()
        for t in range(NT):
            t0 = t * TT
            xt_evs = []
            for s in range(NS):
                xf, fdeps = xfr.next()
                ld = Pg.dma(lambda e, xf=xf, s=s, t0=t0: e.dma_start(out=XF[xf], in_=src[t0 + s * P:t0 + (s + 1) * P, :]),
                            ("xf", xf), deps=fdeps)
                xs, xdeps = xbr.next()
                cast = Pg.op("dve", lambda e, xs=xs, xf=xf: e.tensor_copy(out=XB[xs], in_=XF[xf]), deps=[ld] + xdeps)
                xfr.read(xf, cast)
                evs = transposes(XB[xs], s, XT, xt_readers if s == 0 else [], cast, evac_i)
                xbr.read(xs, evs[-1])
                xt_evs += evs
            xt_readers = []
            first = True
            for p0 in range(0, len(qk), HG):
                grp = qk[p0:p0 + HG]
                c0 = grp[0][2]
                assert all(grp[i_][2] == c0 + i_ * P for i_ in range(len(grp)))
                sw, wdeps = wqr.next()
                lw = Pg.dma(lambda e, sw=sw, c0=c0, n=len(grp): e.dma_start(
                    out=WQ[sw][:, :, 0:n * P], in_=w[:, :, c0:c0 + n * P]), ("wq", sw), deps=wdeps)
                for gi, (nm, idx, _) in enumerate(grp):
                    b, bdeps = banks.next()
                    last = None
                    for kc in range(KC):
                        d_ = []
                        if kc == 0:
                            d_ = bdeps + [lw] + (xt_evs if first else [])
                        last = Pg.op("pe", lambda e, b=b, sw=sw, kc=kc, gi=gi: e.matmul(
                            bank(b)[:, 0:TT], WQ[sw][:, kc, gi * P:(gi + 1) * P], XT[:, kc, :],
                            start=(kc == 0), stop=(kc == KC - 1)), deps=d_)
                    first = False
                    qs, qdeps = qsr.next()
                    eng = ("act", "dve")[evac_i[0] % 2]
                    evac_i[0] += 1
                    if eng == "act":
                        ev = Pg.op("act", lambda e, qs=qs, b=b: e.copy(out=QS[qs], in_=bank(b)[:, 0:TT]), deps=[last] + qdeps)
                    else:
                        ev = Pg.op("dve", lambda e, qs=qs, b=b: e.tensor_copy(out=QS[qs], in_=bank(b)[:, 0:TT]), deps=[last] + qdeps)
                    banks.read(b, ev)
                    dst = (QT if nm == "q" else KTd)[idx]
                    stt = Pg.dma(lambda e, qs=qs, dst=dst, t0=t0: e.dma_start(out=dst[:, t0:t0 + TT], in_=QS[qs]),
                                 ("st_qk", qs), deps=[ev], eng="act")
                    qsr.read(qs, stt)
                wqr.read(sw, last)
                pc[0] += 1
                pump_to(f0 + (f1 - f0) * pc[0] / (npc * NT))
            for (sc0, dc0, wdt) in vblocks:
                sw, wdeps = wvr.next()
                lw = Pg.dma(lambda e, sw=sw, sc0=sc0, wdt=wdt: e.dma_start(out=WV[sw][:, :, 0:wdt], in_=w[:, :, sc0:sc0 + wdt]),
                            ("wv", sw), deps=wdeps)
                for s in range(NS):
                    b, bdeps = banks.next()
                    last = None
                    for kc in range(KC):
                        d_ = (bdeps + [lw]) if kc == 0 else []
                        last = Pg.op("pe", lambda e, b=b, sw=sw, kc=kc, s=s, wdt=wdt: e.matmul(
                            bank(b)[:, 0:wdt], XT[:, kc, s * P:(s + 1) * P], WV[sw][:, kc, 0:wdt],
                            start=(kc == 0), stop=(kc == KC - 1)), deps=d_)
                    vs, vdeps = vsr.next()
                    eng = ("act", "dve")[evac_i[0] % 2]
                    evac_i[0] += 1
                    if eng == "act":
                        ev = Pg.op("act", lambda e, vs=vs, b=b, wdt=wdt: e.copy(out=VS[vs][:, 0:wdt], in_=bank(b)[:, 0:wdt]), deps=[last] + vdeps)
                    else:
                        ev = Pg.op("dve", lambda e, vs=vs, b=b, wdt=wdt: e.tensor_copy(out=VS[vs][:, 0:wdt], in_=bank(b)[:, 0:wdt]), deps=[last] + vdeps)
                    banks.read(b, ev)
                    r0_ = t0 + s * P
                    stt = Pg.dma(lambda e, vs=vs, r0_=r0_, dc0=dc0, wdt=wdt: e.dma_start(
                        out=Vd[r0_:r0_ + P, dc0:dc0 + wdt], in_=VS[vs][:, 0:wdt]), ("st_v", vs), deps=[ev], eng="act")
                    vsr.read(vs, stt)
                wvr.read(sw, last)
                pc[0] += 1
                pump_to(f0 + (f1 - f0) * pc[0] / (npc * NT))
            xt_readers = [last]
        Pg.barrier()

    def na_phase(li, f0=0.5, f1=0.92):
        A.reset()
        j = li // 2
        QH = [A.view(BF16, [TOK]) for _ in range(2)]
        KH = [A.view(BF16, [TOK]) for _ in range(2)]
        VH = [A.view(BF16, [NBLK, P]) for _ in range(2)]
        NB = [A.view(F32, [7 * P]) for _ in range(2)]
        NAMs = A.view(F32, [32 * 6 * 2])
        NPIPE = 3
        SBF = [A.view(F32, [6 * P]) for _ in range(NPIPE)]
        PT = [A.view(BF16, [6, P]) for _ in range(NPIPE)]
        RC = [A.view(F32, [P]) for _ in range(2)]
        OTH = [A.view(BF16, [TOK]) for _ in range(2)]
        assert A.off <= CVB, A.off
        hr, sbr, ptr_, rcr, otr = Ring(2), Ring(NPIPE), Ring(NPIPE), Ring(2), Ring(2)
        lnam = Pg.dma(lambda e: e.dma_start(out=NAMs, in_=nam), "nam")
        Vv = Vd.rearrange("(b p) c -> p b c", p=P)
        heads = {}

        def head_setup(h):
            hs, hdeps = hr.next()
            Pg.dma(lambda e: e.dma_start(out=QH[hs], in_=QT[h]), ("nah", hs), deps=hdeps)
            Pg.dma(lambda e: e.dma_start(out=KH[hs], in_=KTd[h]), ("nah", hs))
            Pg.dma(lambda e: e.dma_start(out=NB[hs], in_=nab[j, h]), ("nah", hs))
            lh = None
            for b0 in range(0, NBLK, 8):
                lh = Pg.dma(lambda e, b0=b0: e.dma_start(out=VH[hs][:, b0:b0 + 8, :], in_=Vv[:, b0:b0 + 8, h * P:(h + 1) * P]),
                            ("nah", hs))
            os_, odeps = otr.next()
            return dict(hs=hs, lh=lh, os_=os_, odeps=odeps)

        def stage_a(h, i):
            hc = heads[h]
            hs = hc["hs"]
            blocks = NA_TILES[i]
            n = len(blocks)
            na_ = (n + 1) // 2
            groups = [list(range(0, na_)), list(range(na_, n))]
            sb, sbdeps = sbr.next()
            pt, ptdeps = ptr_.next()
            exps = []
            for gi, ks in enumerate(groups):
                b, bdeps = banks.next()
                last = None
                for kk, k in enumerate(ks):
                    jb = blocks[k]
                    d_ = (bdeps + ([hc["lh"]] if i == 0 else [])) if kk == 0 else []
                    last = Pg.op("pe", lambda e, b=b, kk=kk, jb=jb: e.matmul(
                        bank(b)[:, kk * P:(kk + 1) * P], KH[hs][:, jb * P:(jb + 1) * P], QH[hs][:, i * P:(i + 1) * P],
                        start=True, stop=True), deps=d_)
                dl0 = blocks[ks[0]] - i + 3
                w_ = len(ks) * P
                k0 = ks[0]
                ba = Pg.op("dve", lambda e, b=b, k0=k0, w_=w_, dl0=dl0: e.scalar_tensor_tensor(
                    out=SBF[sb][:, k0 * P:k0 * P + w_], in0=bank(b)[:, 0:w_], scalar=SCALE,
                    in1=NB[hs][:, dl0 * P:dl0 * P + w_], op0=ALU.mult, op1=ALU.add),
                    deps=[last] + (sbdeps if gi == 0 else []))
                banks.read(b, ba)
                for k in ks:
                    xd = [ba] + (ptdeps if not exps else []) + ([lnam] if (h == 0 and i == 0) else [])
                    if na_full_valid(i, k):
                        exps.append(Pg.op("act", lambda e, k=k: e.activation(
                            out=PT[pt][:, k, :], in_=SBF[sb][:, k * P:(k + 1) * P], func=AF.Exp), deps=xd))
                    else:
                        for qrl in (0, 1):
                            col = (i * 6 + k) * 2 + qrl
                            exps.append(Pg.op("act", lambda e, k=k, qrl=qrl, col=col: e.activation(
                                out=PT[pt][:, k, qrl * 64:(qrl + 1) * 64],
                                in_=SBF[sb][:, k * P + qrl * 64:k * P + (qrl + 1) * 64],
                                func=AF.Exp, bias=NAMs[:, col:col + 1], scale=1.0), deps=xd))
            sbr.read(sb, exps[-1])
            return dict(pt=pt, ex=exps[-1], blocks=blocks)

        def stage_b(h, i, a):
            hc = heads[h]
            hs, os_ = hc["hs"], hc["os_"]
            pt, blocks = a["pt"], a["blocks"]
            n = len(blocks)
            b, bdeps = banks.next()
            last = None
            for k in range(n):
                jb = blocks[k]
                last = Pg.op("pe", lambda e, k=k, jb=jb: e.matmul(
                    bank(b)[:, 0:P], VH[hs][:, jb, :], PT[pt][:, k, :], start=(k == 0), stop=(k == n - 1)),
                    deps=(bdeps + [a["ex"]]) if k == 0 else [])
            for k in range(n):
                last = Pg.op("pe", lambda e, k=k: e.matmul(
                    bank(b)[:, P:2 * P], ones, PT[pt][:, k, :], start=(k == 0), stop=(k == n - 1)))
            ptr_.read(pt, last)
            rc, rcdeps = rcr.next()
            r1_ = Pg.op("dve", lambda e: e.reciprocal(out=RC[rc], in_=bank(b)[:, P:2 * P]), deps=[last] + rcdeps)
            r2_ = Pg.op("dve", lambda e: e.tensor_tensor(
                out=OTH[os_][:, i * P:(i + 1) * P], in0=bank(b)[:, 0:P], in1=RC[rc], op=ALU.mult),
                deps=[r1_] + (hc["odeps"] if i == 0 else []))
            banks.read(b, r2_)
            rcr.read(rc, r2_)
            if i == 31:
                hr.read(hs, last)
                stt = Pg.dma(lambda e: e.dma_start(out=OT[h * P:(h + 1) * P, :], in_=OTH[os_]), ("st_ot", os_), deps=[r2_], eng="act")
                otr.read(os_, stt)

        items = [(h, i) for h in range(NAH) for i in range(32)]
        prev = None
        for n_, (h, i) in enumerate(items):
            if i == 0:
                heads[h] = head_setup(h)
            a = stage_a(h, i)
            if prev is not None:
                stage_b(*prev)
            prev = (h, i, a)
            pump_to(f0 + (f1 - f0) * (n_ + 1) / len(items))
        stage_b(*prev)
        Pg.barrier()

    def dil_phase(li, f0=0.5, f1=0.92):
        A.reset()
        PAD = 1024
        QD = [A.view(BF16, [TOK]) for _ in range(2)]
        KD = [A.view(BF16, [PAD + TOK + PAD]) for _ in range(2)]
        VMAX = max(d * (TOK // (P * d) + 1) for _, d in DIL)
        VD = [A.view(BF16, [VMAX, P]) for _ in range(2)]
        DB_ = [A.view(F32, [2 * P]) for _ in range(2)]
        NPIPE = 3
        SBF = [A.view(F32, [2 * P]) for _ in range(NPIPE)]
        PT = [A.view(BF16, [2, P]) for _ in range(NPIPE)]
        UD = A.view(F32, [2, TOK])
        OTD = [A.view(BF16, [TOK]) for _ in range(2)]
        assert A.off <= CVB, A.off
        hr, sbr, ptr_, otr = Ring(2), Ring(NPIPE), Ring(NPIPE), Ring(2)
        zs = []
        for s_ in range(2):
            zs.append(Pg.op("pool", lambda e, s_=s_: e.memset(KD[s_], 0.0)))
            zs.append(Pg.op("pool", lambda e, s_=s_: e.memset(VD[s_], 0.0)))
        st_ = {"ud_readers": [], "ud_last": None, "cnt": 0}
        ctxs = {}

        def group_setup(h, g):
            d = DIL[g][1]
            idx = g * DILH + h
            nblk = TOK // d // P
            hs, hdeps = hr.next()
            zdeps = zs if st_["cnt"] < 2 else []
            st_["cnt"] += 1
            Pg.dma(lambda e: e.dma_start(out=QD[hs], in_=QT[idx]), ("dlh", hs), deps=hdeps + zdeps)
            Pg.dma(lambda e: e.dma_start(out=KD[hs][:, PAD:PAD + TOK], in_=KTd[idx]), ("dlh", hs))
            Pg.dma(lambda e: e.dma_start(out=DB_[hs], in_=dbt[idx]), ("dlh", hs))
            c0 = g * DW + h * P
            Vg = Vd[:, c0:c0 + P]
            VDv = VD[hs][:, 0:d * (nblk + 1), :].rearrange("p (r m) c -> p r m c", r=d)
            Pg.dma(lambda e: e.dma_start(
                out=VDv[64:128, :, 0, :], in_=Vg[0:64 * d, :].rearrange("(p r) c -> p r c", r=d)), ("dlh", hs))
            Pg.dma(lambda e: e.dma_start(
                out=VDv[0:64, :, nblk, :], in_=Vg[TOK - 64 * d:TOK, :].rearrange("(p r) c -> p r c", r=d)), ("dlh", hs))
            lh = None
            full = Vg[64 * d:64 * d + (nblk - 1) * P * d, :].rearrange("(m p r) c -> p r m c", p=P, r=d)
            for r in range(d):
                lh = Pg.dma(lambda e, r=r: e.dma_start(out=VDv[:, r, 1:nblk, :], in_=full[:, r, :, :]), ("dlh", hs))
            return dict(hs=hs, lh=lh, VDv=VDv, d=d, nblk=nblk)

        def stage_a(h, g, r, b_):
            c = ctxs[(h, g)]
            hs, d, nblk = c["hs"], c["d"], c["nblk"]
            qs0 = r + d * P * b_
            qsl = slice(qs0, qs0 + (P - 1) * d + 1, d)
            ka0 = PAD + r + 64 * d * (2 * b_ - 1)
            kb0 = PAD + r + 64 * d * (2 * b_ + 1)
            ksa = slice(ka0, ka0 + (P - 1) * d + 1, d)
            ksb = slice(kb0, kb0 + (P - 1) * d + 1, d)
            b, bdeps = banks.next()
            Pg.op("pe", lambda e: e.matmul(bank(b)[:, 0:P], KD[hs][:, ksa], QD[hs][:, qsl], start=True, stop=True),
                  deps=bdeps + ([c["lh"]] if (r == 0 and b_ == 0) else []))
            last = Pg.op("pe", lambda e: e.matmul(bank(b)[:, P:2 * P], KD[hs][:, ksb], QD[hs][:, qsl], start=True, stop=True))
            sb, sbdeps = sbr.next()
            ba = Pg.op("dve", lambda e: e.scalar_tensor_tensor(
                out=SBF[sb], in0=bank(b)[:, 0:2 * P], scalar=SCALE, in1=DB_[hs], op0=ALU.mult, op1=ALU.add),
                deps=[last] + sbdeps)
            banks.read(b, ba)
            cola = 1 if b_ == 0 else (3 if b_ == nblk // 2 else 0)
            colb = 2 if b_ == nblk - 1 else (4 if b_ == nblk // 2 - 1 else 0)
            pt, ptdeps = ptr_.next()
            if cola == 0 and colb == 0:
                ex = Pg.op("act", lambda e: e.activation(
                    out=PT[pt].rearrange("p a b -> p (a b)"), in_=SBF[sb], func=AF.Exp), deps=[ba] + ptdeps)
            else:
                Pg.op("act", lambda e: e.activation(
                    out=PT[pt][:, 0, :], in_=SBF[sb][:, 0:P], func=AF.Exp, bias=em_sb[:, cola:cola + 1], scale=1.0),
                    deps=[ba] + ptdeps)
                ex = Pg.op("act", lambda e: e.activation(
                    out=PT[pt][:, 1, :], in_=SBF[sb][:, P:2 * P], func=AF.Exp, bias=em_sb[:, colb:colb + 1], scale=1.0))
            sbr.read(sb, ex)
            return dict(pt=pt, ex=ex, qsl=qsl)

        def stage_b(h, g, r, b_, a):
            c = ctxs[(h, g)]
            hs, d, nblk, VDv = c["hs"], c["d"], c["nblk"], c["VDv"]
            pt, qsl = a["pt"], a["qsl"]
            b2, b2deps = banks.next()
            Pg.op("pe", lambda e: e.matmul(bank(b2)[:, 0:P], VDv[:, r, b_, :], PT[pt][:, 0, :], start=True, stop=False),
                  deps=b2deps + [a["ex"]])
            Pg.op("pe", lambda e: e.matmul(bank(b2)[:, 0:P], VDv[:, r, b_ + 1, :], PT[pt][:, 1, :], start=False, stop=True))
            Pg.op("pe", lambda e: e.matmul(bank(b2)[:, P:2 * P], ones, PT[pt][:, 0, :], start=True, stop=False))
            lastpe = Pg.op("pe", lambda e: e.matmul(bank(b2)[:, P:2 * P], ones, PT[pt][:, 1, :], start=False, stop=True))
            ptr_.read(pt, lastpe)
            src2 = bank(b2)[:, 0:2 * P].rearrange("p (a b) -> p a b", a=2)
            dst2 = UD[:, :, qsl]
            if g == 0:
                ac = Pg.op("act", lambda e: e.copy(out=dst2, in_=src2), deps=[lastpe] + st_["ud_readers"])
                st_["ud_readers"] = []
            else:
                ac = Pg.op("dve", lambda e: e.tensor_tensor(out=dst2, in0=src2, in1=dst2, op=ALU.add),
                           deps=[lastpe, st_["ud_prev_group"]])
            banks.read(b2, ac)
            if r == d - 1 and b_ == nblk - 1:
                hr.read(hs, lastpe)
                st_["ud_prev_group"] = ac
                if g == 2:
                    os_, odeps = otr.next()
                    r1_ = Pg.op("dve", lambda e: e.reciprocal(out=UD[:, 1, :], in_=UD[:, 1, :]), deps=[ac])
                    r2_ = Pg.op("dve", lambda e: e.tensor_tensor(out=OTD[os_], in0=UD[:, 0, :], in1=UD[:, 1, :], op=ALU.mult),
                                deps=[r1_] + odeps)
                    st_["ud_readers"] = [r2_]
                    stt = Pg.dma(lambda e: e.dma_start(out=OT[h * P:(h + 1) * P, :], in_=OTD[os_]), ("st_ot", os_), deps=[r2_], eng="act")
                    otr.read(os_, stt)

        items = []
        for h in range(DILH):
            for g, (_, d) in enumerate(DIL):
                for r in range(d):
                    for b_ in range(TOK // d // P):
                        items.append((h, g, r, b_))
        prev = None
        for n_, (h, g, r, b_) in enumerate(items):
            if r == 0 and b_ == 0:
                ctxs[(h, g)] = group_setup(h, g)
            a = stage_a(h, g, r, b_)
            if prev is not None:
                stage_b(*prev)
            prev = (h, g, r, b_, a)
            pump_to(f0 + (f1 - f0) * (n_ + 1) / len(items))
        stage_b(*prev)
        Pg.barrier()

    def out_phase(li):
        A.reset()
        mt, j = li % 2, li // 2
        nC = NAH if mt == 0 else DILH
        wo = (wbf["na_w_o"] if mt == 0 else wbf["dil_w_o"])[j].rearrange("(c p) d -> p c d", p=P)
        OTS = [A.view(BF16, [nC, TT]) for _ in range(2)]
        WO = [A.view(BF16, [nC, DB]) for _ in range(2)]
        YACCS = [A.view(F32, [NS, D]) for _ in range(2)]
        yring = Ring(2)
        XO = [A.view(F32, [D]) for _ in range(1)]
        LNG = A.view(F32, [D])
        LNB = A.view(F32, [D])
        ST = [A.view(F32, [24]) for _ in range(2)]
        MV = [A.view(F32, [2]) for _ in range(2)]
        SD = [A.view(F32, [1]) for _ in range(2)]
        RS = [A.view(F32, [1]) for _ in range(2)]
        otr, wor, xor_, smr = Ring(2), Ring(2), Ring(1), Ring(2)
        assert A.off <= CVB, A.off
        lnrec = load_ln(li * 3 + 1, LNG, LNB, [])
        OTv = OT.rearrange("(c p) t -> p c t", p=P)
        yacc_readers = []
        for t in range(NT):
            t0 = t * TT
            ys, ydeps = yring.next()
            YACC = YACCS[ys]
            lds = []
            for s in range(NS):
                lds.append(Pg.dma(lambda e, s=s, t0=t0, YACC=YACC: e.dma_start(out=YACC[:, s, :], in_=y[t0 + s * P:t0 + (s + 1) * P, :]),
                                  ("yacc", ys), deps=ydeps if s == 0 else ()))
            scl = []
            for s in range(NS):
                scl.append(Pg.op("act", lambda e, s=s, YACC=YACC: e.mul(out=YACC[:, s, :], in_=YACC[:, s, :], mul=ALPHA), deps=[lds[-1]]))
            os_, odeps = otr.next()
            lo = Pg.dma(lambda e, os_=os_, t0=t0: e.dma_start(out=OTS[os_][:, 0:nC, :], in_=OTv[:, 0:nC, t0:t0 + TT]), ("ots", os_), deps=odeps)
            yacc_ready = [[] for _ in range(NS)]
            last = None
            for db in range(NDB):
                sw, wdeps = wor.next()
                lw = Pg.dma(lambda e, sw=sw, db=db: e.dma_start(out=WO[sw], in_=wo[:, :, db * DB:(db + 1) * DB]), ("wo", sw), deps=wdeps)
                for s in range(NS):
                    b, bdeps = banks.next()
                    for c in range(nC):
                        last = Pg.op("pe", lambda e, b=b, c=c, s=s, os_=os_, sw=sw: e.matmul(
                            bank(b)[:, 0:DB], OTS[os_][:, c, s * P:(s + 1) * P], WO[sw][:, c, :],
                            start=(c == 0), stop=(c == nC - 1)), deps=(bdeps + [lw, lo]) if c == 0 else [])
                    ev = Pg.op("dve", lambda e, b=b, s=s, db=db, YACC=YACC: e.scalar_tensor_tensor(
                        out=YACC[:, s, db * DB:(db + 1) * DB], in0=bank(b)[:, 0:DB], scalar=1.0,
                        in1=YACC[:, s, db * DB:(db + 1) * DB], op0=ALU.mult, op1=ALU.add), deps=[last, scl[s]])
                    banks.read(b, ev)
                    yacc_ready[s] = [ev]
                wor.read(sw, last)
            otr.read(os_, last)
            for s in range(NS):
                yacc_ready[s] = yacc_ready[s] + [lnrec]
            for s in range(NS):
                yring.read(ys, ln_sub(t, s, YACC[:, s, :], yacc_ready[s], XO, xor_, LNG, LNB, ST, MV, SD, RS, smr))
            pump_to(0.92 + 0.08 * (t + 1) / NT)
        if pumps["cur"] is not None:
            pumps["cur"].drain()
        Pg.barrier()

    for li in range(DEPTH):
        ffn_phase(li, 0)
        qkv_phase(li)
        if li % 2 == 0:
            na_phase(li)
        else:
            dil_phase(li)
        out_phase(li)
        ffn_phase(li, 1)

    Pg.finalize()
    Pg.emit(nc)
    st.close()
    return nc


def run_cores(cfg, xs, samples, weights):
    nc = build_program(cfg)
    nabt = make_nab(np.asarray(weights["na_rpb"], np.float32)).reshape(cfg.NNA, cfg.NAH, P, 7 * P)
    dbtt = make_dbt(np.asarray(weights["rel_bias"], np.float32), cfg.DILH).reshape(3 * cfg.DILH, P, 2 * P)
    cstt = np.concatenate([np.eye(P, dtype=np.float32), np.ones((P, P), np.float32)], axis=1)
    common = {
        "ln_g": np.ascontiguousarray(np.asarray(weights["ln_g"], np.float32).reshape(cfg.DEPTH * 3, cfg.D)),
        "ln_b": np.ascontiguousarray(np.asarray(weights["ln_b"], np.float32).reshape(cfg.DEPTH * 3, cfg.D)),
        "nab": nabt, "dbt": dbtt, "cst": cstt,
    }
    for k in ("ffn_w_gate", "ffn_w_up", "ffn_w_down", "na_w_qkv", "na_w_o", "dil_w_qkv", "dil_w_o"):
        common[k] = np.ascontiguousarray(np.asarray(weights[k], np.float32))
    nams = {False: make_nam(False), True: make_nam(True)}
    ems = {False: make_em(False), True: make_em(True)}
    in_maps = []
    for c in range(len(xs)):
        m = dict(common)
        m["x"] = np.ascontiguousarray(xs[c], np.float32)
        m["nam"] = nams[samples[c]]
        m["em"] = ems[samples[c]]
        in_maps.append(m)
    res = run_bass_kernel_spmd(nc, in_maps, core_ids=list(range(len(xs))))
    return [np.asarray(r["y"]) for r in res.results]


def kernel(x_prompt, x_sample, ln_g, ln_b, ffn_w_gate, ffn_w_up, ffn_w_down, na_w_qkv, na_w_o,
           na_rpb, dil_w_qkv, dil_w_o, rel_bias):
    cfg = Cfg()
    x_prompt = np.asarray(x_prompt, np.float32)
    x_sample = np.asarray(x_sample, np.float32)
    xs, samples = [], []
    for c in range(4):
        xs.append(x_prompt[2 * c:2 * c + 2].reshape(cfg.TOK, cfg.D))
        samples.append(False)
    for c in range(4):
        xs.append(x_sample[c])
        samples.append(True)
    weights = dict(ln_g=ln_g, ln_b=ln_b, ffn_w_gate=ffn_w_gate, ffn_w_up=ffn_w_up, ffn_w_down=ffn_w_down,
                   na_w_qkv=na_w_qkv, na_w_o=na_w_o, na_rpb=na_rpb, dil_w_qkv=dil_w_qkv, dil_w_o=dil_w_o,
                   rel_bias=rel_bias)
    ys = run_cores(cfg, xs, samples, weights)
    y_prompt = np.stack([ys[c].reshape(2, 2048, cfg.D) for c in range(4)]).reshape(8, 2048, cfg.D)
    y_sample = np.stack(ys[4:8])
    return (y_prompt.astype(np.float32), y_sample.astype(np.float32))
```

```python
import math
from contextlib import ExitStack

import numpy as np
import concourse.bass as bass
import concourse.mybir as mybir
from concourse.bass_utils import run_bass_kernel_spmd

F32 = mybir.dt.float32
BF16 = mybir.dt.bfloat16
AF = mybir.ActivationFunctionType
ALU = mybir.AluOpType

P = 128
NEG = -1.0e30
DIL = ((128, 1), (512, 4), (2048, 16))
LN_EPS = 1e-5


class Cfg:
    def __init__(self, D=2048, FF=5632, DEPTH=4, TOK=4096):
        self.D, self.FF, self.DEPTH, self.TOK = D, FF, DEPTH, TOK
        self.KC = D // P
        self.FC = FF // P
        self.NAH = D // P
        self.DILH = D // (2 * P)
        self.DW = D // 2
        self.NNA = (DEPTH + 1) // 2
        self.NDIL = DEPTH // 2
        self.TT = 512
        self.NT = TOK // self.TT
        self.NS = self.TT // P
        self.DB = min(512, D)
        self.NDB = D // self.DB
        self.ALPHA = (2 * DEPTH) ** 0.25
        self.PF = 4 if self.FC % 4 == 0 else self.FC
        self.NPI = self.FC // self.PF


def na_start(r, sample):
    if sample:
        return min(max(r - 4, 0), 56)
    base = (r // 32) * 32
    return base + min(max(r % 32 - 4, 0), 24)


def na_valid(qr, kr, sample):
    s = na_start(qr, sample)
    return s <= kr < s + 8


def na_tiles():
    tiles = []
    for i in range(32):
        ks = set()
        for sample in (False, True):
            for qr in (2 * i, 2 * i + 1):
                s = na_start(qr, sample)
                for kr in range(s, s + 8):
                    ks.add(kr // 2)
        lo, hi = min(ks), max(ks)
        assert hi - lo + 1 <= 6 and abs(lo - i) <= 3 and abs(hi - i) <= 3
        tiles.append(list(range(lo, hi + 1)))
    return tiles


NA_TILES = na_tiles()


def na_full_valid(i, k):
    j = NA_TILES[i][k]
    for sample in (False, True):
        for qrl in (0, 1):
            for krl in (0, 1):
                if not na_valid(2 * i + qrl, 2 * j + krl, sample):
                    return False
    return True


def make_nam(sample):
    m = np.zeros((P, 32 * 6 * 2), np.float32)
    for i in range(32):
        for k, j in enumerate(NA_TILES[i]):
            for qrl in (0, 1):
                for krl in (0, 1):
                    if not na_valid(2 * i + qrl, 2 * j + krl, sample):
                        m[krl * 64:(krl + 1) * 64, (i * 6 + k) * 2 + qrl] = NEG
    return m


def make_em(sample):
    e = np.zeros((P, 8), np.float32)
    e[:64, 1] = NEG
    e[64:, 2] = NEG
    if not sample:
        e[:64, 3] = NEG
        e[64:, 4] = NEG
    return e


def make_nab(rpb):
    L, H = rpb.shape[:2]
    krl = np.arange(2)[:, None, None, None, None]
    kc = np.arange(64)[None, :, None, None, None]
    dl = np.arange(7)[None, None, :, None, None] - 3
    qrl = np.arange(2)[None, None, None, :, None]
    qc = np.arange(64)[None, None, None, None, :]
    dr = 2 * dl + krl - qrl
    dr_ok = np.abs(dr) <= 7
    dri = np.clip(dr + 7, 0, 14)
    dc = np.clip(kc - qc, -15, 15) + 15
    cs = np.clip(qc - 8, 0, 48)
    ok = (kc >= cs) & (kc < cs + 16) & dr_ok
    dri, dc, ok = np.broadcast_arrays(dri, dc, ok)
    dr_b = np.broadcast_to(dr, ok.shape)
    g = rpb[:, :, dri, dc]
    out = np.where(ok[None, None], g, np.float32(NEG)).astype(np.float32).reshape(L, H, P, 7 * P)
    oki = ok & (dr_b >= -4) & (dr_b <= 3)
    outi = np.where(oki[None, None], g, np.float32(NEG)).astype(np.float32).reshape(L, H, P, 7 * P)
    return np.ascontiguousarray(np.stack([out, outi], axis=3))


def na_interior(i):
    if not ((2 <= i <= 13) or (18 <= i <= 29)):
        return False
    if NA_TILES[i] != list(range(i - 2, i + 3)):
        return False
    for k, j in enumerate(NA_TILES[i]):
        for sample in (False, True):
            for qrl in (0, 1):
                for krl in (0, 1):
                    dr = 2 * j + krl - (2 * i + qrl)
                    if na_valid(2 * i + qrl, 2 * j + krl, sample) != (-4 <= dr <= 3):
                        return False
    return True


def t5_bucket_np(rel):
    half, max_exact = 16, 8
    sign = np.where(rel > 0, half, 0)
    n = np.abs(rel)
    nf = np.maximum(n, 1).astype(np.float32)
    large = max_exact + (np.log(nf / np.float32(max_exact)) / np.float32(math.log(1024 / max_exact))
                         * np.float32(half - max_exact)).astype(np.int32)
    large = np.minimum(large, half - 1)
    return sign + np.where(n < max_exact, n, large)


def make_dbt(rel_bias, DILH):
    out = np.full((3 * DILH, P, 2, P), NEG, np.float32)
    p = np.arange(P)[:, None]
    q = np.arange(P)[None, :]
    for g, (_, d) in enumerate(DIL):
        for ab in (0, 1):
            rel = p + (-64 if ab == 0 else 64) - q
            ok = np.abs(rel) <= 64
            bk = t5_bucket_np(rel * d)
            for h in range(DILH):
                vals = rel_bias[bk, g * DILH + h]
                out[g * DILH + h, :, ab, :] = np.where(ok, vals, np.float32(NEG))
    return out


class Rec:
    __slots__ = ("eng", "fn", "deps", "sig", "val", "key", "isdma")


class Ring:
    def __init__(self, n):
        self.n = n
        self.i = 0
        self.readers = [[] for _ in range(n)]

    def next(self):
        s = self.i % self.n
        self.i += 1
        deps = self.readers[s]
        self.readers[s] = []
        return s, deps

    def read(self, s, rec):
        self.readers[s].append(rec)


class Prog:
    ENGS = ("pe", "act", "dve", "pool", "sp")

    def __init__(self):
        self.q = {e: [] for e in self.ENGS}
        self.last_dma = {}
        self.deferred = []

    def op(self, eng, fn, deps=()):
        r = Rec()
        r.eng, r.fn, r.sig, r.val, r.key, r.isdma = eng, fn, False, None, None, False
        r.deps = [d for d in deps if d is not None]
        self.q[eng].append(r)
        return r

    def dma(self, fn, key, deps=(), eng="sp"):
        r = self.op(eng, fn, deps)
        r.isdma, r.key, r.sig = True, key, True
        self.last_dma[key] = r
        return r

    def defer(self, fn, key, deps=()):
        self.deferred.append((fn, key, list(deps)))

    def flush(self):
        for fn, key, deps in self.deferred:
            self.dma(fn, key, deps)
        self.deferred = []

    def barrier(self):
        self.flush()
        lasts = []
        for e in self.ENGS:
            for r in reversed(self.q[e]):
                if r.fn is not None and not r.isdma:
                    lasts.append(r)
                    break
        lasts += list(self.last_dma.values())
        self.last_dma = {}
        for e in self.ENGS:
            self.op(e, None, deps=[l for l in lasts if not (l.eng == e and not l.isdma)])

    def finalize(self):
        for e in self.ENGS:
            for r in self.q[e]:
                for d in r.deps:
                    assert d.fn is not None
                    d.sig = True
        cnt = {}
        for e in self.ENGS:
            for r in self.q[e]:
                if r.sig:
                    if not r.isdma:
                        r.key = ("eng", e)
                    cnt[r.key] = cnt.get(r.key, 0) + (16 if r.isdma else 1)
                    r.val = cnt[r.key]
        self.keys = list(cnt.keys())
        self.maxvals = cnt

    def emit(self, nc):
        hmap = {"pe": nc.tensor, "act": nc.scalar, "dve": nc.vector, "pool": nc.gpsimd, "sp": nc.sync}
        bmap = {"pe": "tensor", "act": "scalar", "dve": "vector", "pool": "gpsimd", "sp": "sync"}
        with ExitStack() as st:
            sems = {}
            for i, k in enumerate(self.keys):
                sems[k] = st.enter_context(nc.semaphore("s%d" % i))
            block = st.enter_context(nc.Block())
            for e in self.ENGS:
                def body(engine, e=e):
                    waited = {}
                    for r in self.q[e]:
                        for d in r.deps:
                            if waited.get(d.key, 0) < d.val:
                                engine.wait_ge(sems[d.key], d.val)
                                waited[d.key] = d.val
                        if r.fn is None:
                            continue
                        ins = r.fn(engine)
                        if r.sig:
                            ins.then_inc(sems[r.key], 16 if r.isdma else 1)
                getattr(block, bmap[e])(body)


def build_program(cfg):
    D, FF, DEPTH, TOK = cfg.D, cfg.FF, cfg.DEPTH, cfg.TOK
    KC, FC, NAH, DILH, DW = cfg.KC, cfg.FC, cfg.NAH, cfg.DILH, cfg.DW
    TT, NT, NS, DB, NDB = cfg.TT, cfg.NT, cfg.NS, cfg.DB, cfg.NDB
    PF, NPI = cfg.PF, cfg.NPI
    ALPHA = cfg.ALPHA
    NBLK = TOK // P
    SCALE = float(P) ** -0.5

    nc = bass.Bass("TRN2", target_bir_lowering=False)

    def dram(name, shape, dt, kind):
        return nc.dram_tensor(name, list(shape), dt, kind=kind).ap()

    x_in = dram("x", [TOK, D], F32, "ExternalInput")
    y = dram("y", [TOK, D], F32, "ExternalOutput")
    wshapes = {
        "ffn_w_gate": [DEPTH, 2, D, FF], "ffn_w_up": [DEPTH, 2, D, FF], "ffn_w_down": [DEPTH, 2, FF, D],
        "na_w_qkv": [cfg.NNA, D, 3 * D], "na_w_o": [cfg.NNA, D, D],
        "dil_w_qkv": [max(cfg.NDIL, 1), D, 9 * DW], "dil_w_o": [max(cfg.NDIL, 1), DW, D],
    }
    w32 = {k: dram(k, s, F32, "ExternalInput") for k, s in wshapes.items()}
    wbf = {k: dram(k + "_bf", s, BF16, "Internal") for k, s in wshapes.items()}
    lng = dram("ln_g", [DEPTH * 3, D], F32, "ExternalInput")
    lnb = dram("ln_b", [DEPTH * 3, D], F32, "ExternalInput")
    nab = dram("nab", [cfg.NNA, NAH, P, 2 * 7 * P], F32, "ExternalInput")
    nam = dram("nam", [P, 32 * 6 * 2], F32, "ExternalInput")
    dbt = dram("dbt", [3 * DILH, P, 2 * P], F32, "ExternalInput")
    em = dram("em", [P, 8], F32, "ExternalInput")
    cst = dram("cst", [P, 2 * P], F32, "ExternalInput")
    NQK = max(NAH, 3 * DILH)
    VW = max(D, 3 * DW)
    QT = dram("qt_s", [NQK, P, TOK], BF16, "Internal")
    KTd = dram("kt_s", [NQK, P, TOK], BF16, "Internal")
    Vd = dram("v_s", [TOK, VW], BF16, "Internal")
    OT = dram("ot_s", [D, TOK], BF16, "Internal")

    ARENA_BYTES = 204 * 1024
    st = ExitStack()
    arena = st.enter_context(nc.sbuf_tensor("arena", [P, ARENA_BYTES // 2], BF16))
    ps = st.enter_context(nc.psum_tensor("ps", [P, 8 * 512], F32))

    class Arena:
        def __init__(self):
            self.off = 0
            self.base = 0

        def alloc(self, nbytes):
            o = self.off
            self.off += (nbytes + 63) // 64 * 64
            assert self.off <= ARENA_BYTES, ("SBUF arena overflow", self.off)
            return o

        def view(self, dt, shape):
            n = 1
            for s_ in shape:
                n *= s_
            esz = 4 if dt == F32 else 2
            o = self.alloc(n * esz)
            a = arena[:, o // 2: o // 2 + n * esz // 2]
            if dt == F32:
                a = a.bitcast(F32)
            if len(shape) == 2:
                a = a.rearrange("p (a b) -> p a b", a=shape[0])
            elif len(shape) == 3:
                a = a.rearrange("p (a b c) -> p a b c", a=shape[0], b=shape[1])
            return a

        def mark(self):
            self.base = self.off

        def reset(self):
            self.off = self.base

    A = Arena()
    Pg = Prog()
    banks = Ring(8)

    def bank(b):
        return ps[:, b * 512:(b + 1) * 512]

    cst32 = A.view(F32, [2 * P])
    ident = A.view(BF16, [P])
    ones = A.view(BF16, [P])
    em_sb = A.view(F32, [8])
    A.mark()
    r0 = Pg.dma(lambda e: e.dma_start(out=cst32, in_=cst), "c0")
    r1 = Pg.dma(lambda e: e.dma_start(out=em_sb, in_=em), "c0")
    Pg.op("dve", lambda e: e.tensor_copy(out=ident, in_=cst32[:, 0:P]), deps=[r1])
    Pg.op("dve", lambda e: e.tensor_copy(out=ones, in_=cst32[:, P:2 * P]))

    CH = 4096
    CVB = ARENA_BYTES - 48 * 1024

    def fixed(off, dt, n):
        esz = 4 if dt == F32 else 2
        a_ = arena[:, off // 2: off // 2 + n * esz // 2]
        return a_.bitcast(F32) if dt == F32 else a_

    cvf = [fixed(CVB + i_ * CH * 4, F32, CH) for i_ in range(2)]
    cvb = [fixed(CVB + 2 * CH * 4 + i_ * CH * 2, BF16, CH) for i_ in range(2)]

    def sub_weights(li, part):
        j = li // 2
        if part == "ffn1":
            return [(nm, (li, 0)) for nm in ("ffn_w_gate", "ffn_w_up", "ffn_w_down")]
        if part == "ffn2":
            return [(nm, (li, 1)) for nm in ("ffn_w_gate", "ffn_w_up", "ffn_w_down")]
        if li % 2 == 0:
            return [("na_w_qkv", (j,)), ("na_w_o", (j,))]
        return [("dil_w_qkv", (j,)), ("dil_w_o", (j,))]

    def chunks_of(order, CH=CH):
        out = []
        for nm, idx in order:
            src, dst = w32[nm], wbf[nm]
            for i_ in idx:
                src, dst = src[i_], dst[i_]
            src = src.rearrange("k f -> (k f)").rearrange("(p x) -> p x", p=P)
            dst = dst.rearrange("k f -> (k f)").rearrange("(p x) -> p x", p=P)
            X = src.shape[1]
            for c0 in range(0, X, CH):
                out.append((src, dst, c0, min(CH, X - c0)))
        return out

    def convert_prologue(order):
        NSL = 4
        f32s = [A.view(F32, [CH]) for _ in range(NSL)]
        b16s = [A.view(BF16, [CH]) for _ in range(NSL)]
        lring, sring = Ring(NSL), Ring(NSL)
        for c, (src, dst, c0, n) in enumerate(chunks_of(order)):
            s, ldeps = lring.next()
            ld = Pg.dma(lambda e, s=s, n=n, src=src, c0=c0: e.dma_start(out=f32s[s][:, :n], in_=src[:, c0:c0 + n]),
                        ("cvl", s), deps=ldeps)
            s2, sdeps = sring.next()
            eng = ("dve", "pool", "dve")[c % 3]
            cast = Pg.op(eng, lambda e, s=s, s2=s2, n=n: e.tensor_copy(out=b16s[s2][:, :n], in_=f32s[s][:, :n]),
                         deps=[ld] + sdeps)
            lring.read(s, cast)
            stt = Pg.dma(lambda e, s2=s2, n=n, dst=dst, c0=c0: e.dma_start(out=dst[:, c0:c0 + n], in_=b16s[s2][:, :n]),
                         ("cvs", s2), deps=[cast], eng="act")
            sring.read(s2, stt)

    class Pump:
        def __init__(self, order, cvf, cvb, ch, engs):
            self.chunks = chunks_of(order, ch)
            self.cvf, self.cvb = cvf, cvb
            self.i = 0
            self.loads = {}
            self.lring, self.sring = Ring(2), Ring(2)
            self.engs = engs

        def _load(self, idx):
            src, dst, c0, n = self.chunks[idx]
            s, deps = self.lring.next()
            rec = Pg.dma(lambda e, s=s, n=n, src=src, c0=c0: e.dma_start(out=self.cvf[s][:, :n], in_=src[:, c0:c0 + n]),
                         ("pcl", s), deps=deps, eng="pool")
            self.loads[idx] = (s, rec)

        def pump(self, k):
            for _ in range(k):
                if self.i >= len(self.chunks):
                    return
                if self.i not in self.loads:
                    self._load(self.i)
                if self.i + 1 < len(self.chunks) and (self.i + 1) not in self.loads:
                    self._load(self.i + 1)
                s, ld = self.loads.pop(self.i)
                src, dst, c0, n = self.chunks[self.i]
                s2, sdeps = self.sring.next()
                eng = self.engs[self.i % len(self.engs)]
                if eng == "act":
                    cast = Pg.op("act", lambda e, s=s, s2=s2, n=n: e.copy(out=self.cvb[s2][:, :n], in_=self.cvf[s][:, :n]), deps=[ld] + sdeps)
                else:
                    cast = Pg.op(eng, lambda e, s=s, s2=s2, n=n: e.tensor_copy(out=self.cvb[s2][:, :n], in_=self.cvf[s][:, :n]), deps=[ld] + sdeps)
                self.lring.read(s, cast)
                stt = Pg.dma(lambda e, s2=s2, n=n, dst=dst, c0=c0: e.dma_start(out=dst[:, c0:c0 + n], in_=self.cvb[s2][:, :n]),
                             ("pcs", s2), deps=[cast], eng="pool")
                self.sring.read(s2, stt)
                self.i += 1

        def upto(self, frac):
            tgt = int(math.ceil(frac * len(self.chunks)))
            if tgt > self.i:
                self.pump(tgt - self.i)

        def drain(self):
            self.pump(len(self.chunks) - self.i)

    pumps = {"cur": None}

    def pump_to(frac, engs=None):
        pm = pumps["cur"]
        if pm is not None:
            pm.upto(frac)

    convert_prologue(sub_weights(0, "ffn1"))
    Pg.barrier()
    A.reset()

    state = {"first": True}

    def xsrc():
        return x_in if state["first"] else y

    def transposes(xb, s, XT, xt_wdeps, cast_rec, evac_i):
        evs = []
        per = 8
        for k0 in range(0, KC, per):
            nk = min(per, KC - k0)
            b, bdeps = banks.next()
            pb = bank(b).bitcast(BF16)
            last = None
            for kk in range(nk):
                kc = k0 + kk
                last = Pg.op("pe", lambda e, pb=pb, kk=kk, kc=kc, xb=xb: e.transpose(
                    pb[:, kk * P:(kk + 1) * P], xb[:, kc * P:(kc + 1) * P], ident),
                    deps=([cast_rec] + bdeps) if kk == 0 else ())
            eng = ("act", "dve")[evac_i[0] % 2]
            evac_i[0] += 1
            src = pb[:, 0:nk * P].rearrange("p (a b) -> p a b", a=nk)
            dstv = XT[:, k0:k0 + nk, s * P:(s + 1) * P]
            if eng == "act":
                ev = Pg.op("act", lambda e, src=src, dstv=dstv: e.copy(out=dstv, in_=src), deps=[last] + xt_wdeps)
            else:
                ev = Pg.op("dve", lambda e, src=src, dstv=dstv: e.tensor_copy(out=dstv, in_=src), deps=[last] + xt_wdeps)
            banks.read(b, ev)
            evs.append(ev)
        return evs

    def ln_sub(t, s, YV, ready, XO, xoring, LNG, LNB, ST, MV, SD, RS, smring):
        nch = (D + 511) // 512
        cw = D // nch
        sm, smdeps = smring.next()
        stv = ST[sm]
        last = None
        for c in range(nch):
            last = Pg.op("dve", lambda e, c=c: e.bn_stats(out=stv[:, c * 6:(c + 1) * 6], in_=YV[:, c * cw:(c + 1) * cw]),
                         deps=(ready + smdeps) if c == 0 else ())
        ag = Pg.op("dve", lambda e: e.bn_aggr(out=MV[sm], in_=stv[:, 0:nch * 6]), deps=[last])
        sq = Pg.op("act", lambda e: e.activation(out=SD[sm], in_=MV[sm][:, 1:2], func=AF.Sqrt,
                                                 bias=eps_sb[:, 0:1], scale=1.0), deps=[ag])
        rc = Pg.op("dve", lambda e: e.reciprocal(out=RS[sm], in_=SD[sm]), deps=[sq])
        xo, xodeps = xoring.next()
        p1 = Pg.op("dve", lambda e: e.scalar_tensor_tensor(out=XO[xo], in0=YV, scalar=MV[sm][:, 0:1], in1=LNG,
                                                           op0=ALU.subtract, op1=ALU.mult), deps=[ag] + xodeps)
        p2 = Pg.op("dve", lambda e: e.scalar_tensor_tensor(out=XO[xo], in0=XO[xo], scalar=RS[sm][:, 0:1], in1=LNB,
                                                           op0=ALU.mult, op1=ALU.add), deps=[rc, p1])
        smring.read(sm, p2)
        r0_ = t * TT + s * P
        stt = Pg.dma(lambda e: e.dma_start(out=y[r0_:r0_ + P, :], in_=XO[xo]), "st_x", deps=[p2], eng="act")
        xoring.read(xo, stt)
        return p1

    eps_sb = A.view(F32, [1])
    A.mark()
    Pg.op("dve", lambda e: e.memset(eps_sb, LN_EPS))

    def load_ln(idx, LNG, LNB, deps):
        a = Pg.dma(lambda e: e.dma_start(out=LNG, in_=lng[idx:idx + 1, :].partition_broadcast(P)[:, 0, :]), "ln", deps=deps)
        b = Pg.dma(lambda e: e.dma_start(out=LNB, in_=lnb[idx:idx + 1, :].partition_broadcast(P)[:, 0, :]), "ln", deps=deps)
        return b

    def ffn_phase(li, which):
        A.reset()
        XT = A.view(BF16, [KC, TT])
        HT = A.view(BF16, [FC, TT])
        NWG = 4
        WGU = [A.view(BF16, [KC, 2 * P if FC >= 2 else P]) for _ in range(NWG)]
        NWD = 6
        WD = [A.view(BF16, [PF, DB]) for _ in range(NWD)]
        YACC = A.view(F32, [NS, D])
        XB = [A.view(BF16, [D]) for _ in range(2)]
        XO = [A.view(F32, [D]) for _ in range(1)]
        LNG = A.view(F32, [D])
        LNB = A.view(F32, [D])
        SG = [A.view(F32, [TT]) for _ in range(2)]
        ST = [A.view(F32, [24]) for _ in range(2)]
        MV = [A.view(F32, [2]) for _ in range(2)]
        SD = [A.view(F32, [1]) for _ in range(2)]
        RS = [A.view(F32, [1]) for _ in range(2)]
        wgr, wdr, xbr, xor_, sgr, smr = Ring(NWG), Ring(NWD), Ring(2), Ring(1), Ring(2), Ring(2)
        CHF = 1024
        fcvf = [A.view(F32, [CHF]) for _ in range(2)]
        fcvb = [A.view(BF16, [CHF]) for _ in range(2)]
        if which == 0:
            corder = sub_weights(li, "mix") + sub_weights(li, "ffn2")
        else:
            corder = sub_weights(li + 1, "ffn1") if li + 1 < DEPTH else []
        pm = Pump(corder, fcvf, fcvb, CHF, ("dve",))
        lnrec = load_ln(li * 3 + (0 if which == 0 else 2), LNG, LNB, [])
        G = 2 if FC >= 2 else 1
        GW = G * P
        wg = wbf["ffn_w_gate"][li, which].rearrange("(kc p) f -> p kc f", p=P)
        wu = wbf["ffn_w_up"][li, which].rearrange("(kc p) f -> p kc f", p=P)
        wd = wbf["ffn_w_down"][li, which].rearrange("(fc p) d -> p fc d", p=P)
        src = xsrc()
        yacc_readers = []
        xt_readers = []
        ht_readers = []
        evac_i = [0]
        hold = {"xt_readers": []}

        def front_sub(t_, s):
            xo, xodeps = xor_.next()
            r0_ = t_ * TT + s * P
            ld = Pg.dma(lambda e: e.dma_start(out=XO[xo], in_=src[r0_:r0_ + P, :]), "xstg", deps=xodeps)
            xs, xdeps = xbr.next()
            cast = Pg.op("dve" if s % 2 == 0 else "pool", lambda e: e.tensor_copy(out=XB[xs], in_=XO[xo]),
                         deps=[ld] + xdeps)
            xor_.read(xo, cast)
            evs = transposes(XB[xs], s, XT, hold["xt_readers"], cast, evac_i)
            xbr.read(xs, evs[-1])
            return evs

        xt_evs = []
        for s in range(NS):
            xt_evs += front_sub(0, s)
        ngroups = FC // (2 if FC >= 2 else 1)
        yl_at = min(NS - 1, ngroups - 1)
        yl_at2 = min(NS + 5, ngroups - 1)
        pend = None
        scl = []
        for t in range(NT):
            t0 = t * TT
            ln_readers = []
            first_mm = True
            for g in range(FC // G):
                f0 = g * GW
                sg_, gdeps = wgr.next()
                lg = Pg.dma(lambda e, sg_=sg_, f0=f0: e.dma_start(out=WGU[sg_][:, :, 0:GW], in_=wg[:, :, f0:f0 + GW]),
                            ("wgu", sg_), deps=gdeps)
                su_, udeps = wgr.next()
                lu = Pg.dma(lambda e, su_=su_, f0=f0: e.dma_start(out=WGU[su_][:, :, 0:GW], in_=wu[:, :, f0:f0 + GW]),
                            ("wgu", su_), deps=udeps)
                for fl in range(G):
                    fc = g * G + fl
                    bg, bgd = banks.next()
                    lastg = None
                    for kc in range(KC):
                        d_ = []
                        if kc == 0:
                            d_ = bgd + ([lg] if fl == 0 else []) + (xt_evs if first_mm else [])
                        lastg = Pg.op("pe", lambda e, bg=bg, sg_=sg_, kc=kc, fl=fl: e.matmul(
                            bank(bg)[:, 0:TT], WGU[sg_][:, kc, fl * P:(fl + 1) * P], XT[:, kc, :],
                            start=(kc == 0), stop=(kc == KC - 1)), deps=d_)
                    first_mm = False
                    bu, bud = banks.next()
                    lastu = None
                    for kc in range(KC):
                        d_ = (bud + ([lu] if fl == 0 else [])) if kc == 0 else []
                        lastu = Pg.op("pe", lambda e, bu=bu, su_=su_, kc=kc, fl=fl: e.matmul(
                            bank(bu)[:, 0:TT], WGU[su_][:, kc, fl * P:(fl + 1) * P], XT[:, kc, :],
                            start=(kc == 0), stop=(kc == KC - 1)), deps=d_)
                    if fl == G - 1:
                        wgr.read(sg_, lastg)
                        wgr.read(su_, lastu)
                    ss, sdeps = sgr.next()
                    sl = Pg.op("act", lambda e, ss=ss, bg=bg: e.activation(out=SG[ss], in_=bank(bg)[:, 0:TT], func=AF.Silu),
                               deps=[lastg] + sdeps)
                    banks.read(bg, sl)
                    mu = Pg.op("dve", lambda e, ss=ss, bu=bu, fc=fc: e.tensor_tensor(
                        out=HT[:, fc, :], in0=SG[ss], in1=bank(bu)[:, 0:TT], op=ALU.mult),
                        deps=[sl, lastu] + (ht_readers if (g == 0 and fl == 0) else []))
                    banks.read(bu, mu)
                    sgr.read(ss, mu)
                    ht_last = mu
                    xt_last = lastu
                if pend is not None and g < NS:
                    ln_readers.append(ln_sub(pend[0], g, YACC[:, g, :], pend[1][g], XO, xor_, LNG, LNB, ST, MV, SD, RS, smr))
                if g == yl_at and pend is not None:
                    for s2 in range(g + 1, NS):
                        ln_readers.append(ln_sub(pend[0], s2, YACC[:, s2, :], pend[1][s2], XO, xor_, LNG, LNB, ST, MV, SD, RS, smr))
                pm.upto((t * ngroups + g + 1) / float(NT * ngroups))
                if g == yl_at2:
                    lds = []
                    for s2 in range(NS):
                        lds.append(Pg.dma(lambda e, s2=s2, t0=t0: e.dma_start(out=YACC[:, s2, :], in_=src[t0 + s2 * P:t0 + (s2 + 1) * P, :]),
                                          "yacc", deps=ln_readers if s2 == 0 else ()))
                    scl = []
                    for s2 in range(NS):
                        scl.append(Pg.op("act", lambda e, s2=s2: e.mul(out=YACC[:, s2, :], in_=YACC[:, s2, :], mul=ALPHA), deps=[lds[-1]]))
            hold["xt_readers"] = [xt_last]
            ht_readers = []
            next_evs = []
            yacc_ready = [[] for _ in range(NS)]
            for db in range(NDB):
                if t + 1 < NT:
                    for s in range(NS):
                        if s * NDB // NS == db:
                            next_evs += front_sub(t + 1, s)
                bs = []
                for s in range(NS):
                    bs.append(banks.next())
                lastmm = [None] * NS
                for pi in range(NPI):
                    sw, wdeps = wdr.next()
                    lw = Pg.dma(lambda e, sw=sw, pi=pi, db=db: e.dma_start(
                        out=WD[sw], in_=wd[:, pi * PF:(pi + 1) * PF, db * DB:(db + 1) * DB]), ("wd", sw), deps=wdeps)
                    for s in range(NS):
                        b, bdeps = bs[s]
                        for j_ in range(PF):
                            fc = pi * PF + j_
                            d_ = []
                            if j_ == 0:
                                d_ = [lw]
                                if pi == 0:
                                    d_ = d_ + bdeps + ([ht_last] if (db == 0 and s == 0) else [])
                            lastmm[s] = Pg.op("pe", lambda e, b=b, fc=fc, s=s, sw=sw, j_=j_: e.matmul(
                                bank(b)[:, 0:DB], HT[:, fc, s * P:(s + 1) * P], WD[sw][:, j_, :],
                                start=(fc == 0), stop=(fc == FC - 1)), deps=d_)
                    wdr.read(sw, lastmm[NS - 1])
                for s in range(NS):
                    b = bs[s][0]
                    ev = Pg.op("dve", lambda e, b=b, s=s, db=db: e.scalar_tensor_tensor(
                        out=YACC[:, s, db * DB:(db + 1) * DB], in0=bank(b)[:, 0:DB], scalar=0.5,
                        in1=YACC[:, s, db * DB:(db + 1) * DB], op0=ALU.mult, op1=ALU.add),
                        deps=[lastmm[s], scl[s]])
                    banks.read(b, ev)
                    yacc_ready[s] = [ev]
            ht_readers = [lastmm[NS - 1]]
            for s in range(NS):
                yacc_ready[s] = yacc_ready[s] + [lnrec]
            pend = (t, yacc_ready)
            xt_evs = next_evs
        for s in range(NS):
            ln_sub(pend[0], s, YACC[:, s, :], pend[1][s], XO, xor_, LNG, LNB, ST, MV, SD, RS, smr)
        pm.drain()
        state["first"] = False
        Pg.barrier()

    def qkv_phase(li, f0=0.0, f1=0.5):
        A.reset()
        mt, j = li % 2, li // 2
        XT = A.view(BF16, [KC, TT])
        XF = [A.view(F32, [D]) for _ in range(2)]
        XB = [A.view(BF16, [D]) for _ in range(2)]
        HG = min(2, NAH if mt == 0 else DILH)
        NWQ = 4
        WQ = [A.view(BF16, [KC, HG * P]) for _ in range(NWQ)]
        if mt == 0:
            w = wbf["na_w_qkv"][j].rearrange("(kc p) f -> p kc f", p=P)
            qk = [("q", h, h * P) for h in range(NAH)] + [("k", h, D + h * P) for h in range(NAH)]
            VBW = min(512, D)
            vblocks = [(2 * D + c0, c0, VBW) for c0 in range(0, D, VBW)]
        else:
            w = wbf["dil_w_qkv"][j].rearrange("(kc p) f -> p kc f", p=P)
            qk = []
            for g in range(3):
                for c_, nm in ((0, "q"), (1, "k")):
                    for h in range(DILH):
                        qk.append((nm, g * DILH + h, ((g * 3 + c_) * DILH + h) * P))
            VBW = min(512, DW)
            vblocks = []
            for g in range(3):
                for c0 in range(0, DW, VBW):
                    vblocks.append(((g * 3 + 2) * DW + c0, g * DW + c0, VBW))
        NWV = 2
        WV = [A.view(BF16, [KC, VBW]) for _ in range(NWV)]
        QS = [A.view(BF16, [TT]) for _ in range(4)]
        VS = [A.view(BF16, [VBW]) for _ in range(4)]
        xfr, xbr, wqr, wvr, qsr, vsr = Ring(2), Ring(2), Ring(NWQ), Ring(NWV), Ring(4), Ring(4)
        assert A.off <= CVB, A.off
        npc = (len(qk) + HG - 1) // HG + len(vblocks)
        pc = [0]
        xt_readers = []
        evac_i = [0]
        src = xsrc()
        for t in range(NT):
            t0 = t * TT
            xt_evs = []
            for s in range(NS):
                xf, fdeps = xfr.next()
                ld = Pg.dma(lambda e, xf=xf, s=s, t0=t0: e.dma_start(out=XF[xf], in_=src[t0 + s * P:t0 + (s + 1) * P, :]),
                            ("xf", xf), deps=fdeps)
                xs, xdeps = xbr.next()
                cast = Pg.op("dve", lambda e, xs=xs, xf=xf: e.tensor_copy(out=XB[xs], in_=XF[xf]), deps=[ld] + xdeps)
                xfr.read(xf, cast)
                evs = transposes(XB[xs], s, XT, xt_readers if s == 0 else [], cast, evac_i)
                xbr.read(xs, evs[-1])
                xt_evs += evs
            xt_readers = []
            first = True
            for p0 in range(0, len(qk), HG):
                grp = qk[p0:p0 + HG]
                c0 = grp[0][2]
                assert all(grp[i_][2] == c0 + i_ * P for i_ in range(len(grp)))
                sw, wdeps = wqr.next()
                lw = Pg.dma(lambda e, sw=sw, c0=c0, n=len(grp): e.dma_start(
                    out=WQ[sw][:, :, 0:n * P], in_=w[:, :, c0:c0 + n * P]), ("wq", sw), deps=wdeps)
                for gi, (nm, idx, _) in enumerate(grp):
                    b, bdeps = banks.next()
                    last = None
                    for kc in range(KC):
                        d_ = []
                        if kc == 0:
                            d_ = bdeps + [lw] + (xt_evs if first else [])
                        last = Pg.op("pe", lambda e, b=b, sw=sw, kc=kc, gi=gi: e.matmul(
                            bank(b)[:, 0:TT], WQ[sw][:, kc, gi * P:(gi + 1) * P], XT[:, kc, :],
                            start=(kc == 0), stop=(kc == KC - 1)), deps=d_)
                    first = False
                    qs, qdeps = qsr.next()
                    eng = ("act", "dve")[evac_i[0] % 2]
                    evac_i[0] += 1
                    if eng == "act":
                        ev = Pg.op("act", lambda e, qs=qs, b=b: e.copy(out=QS[qs], in_=bank(b)[:, 0:TT]), deps=[last] + qdeps)
                    else:
                        ev = Pg.op("dve", lambda e, qs=qs, b=b: e.tensor_copy(out=QS[qs], in_=bank(b)[:, 0:TT]), deps=[last] + qdeps)
                    banks.read(b, ev)
                    dst = (QT if nm == "q" else KTd)[idx]
                    stt = Pg.dma(lambda e, qs=qs, dst=dst, t0=t0: e.dma_start(out=dst[:, t0:t0 + TT], in_=QS[qs]),
                                 ("st_qk", qs), deps=[ev], eng="act")
                    qsr.read(qs, stt)
                wqr.read(sw, last)
                pc[0] += 1
                pump_to(f0 + (f1 - f0) * pc[0] / (npc * NT))
            for (sc0, dc0, wdt) in vblocks:
                sw, wdeps = wvr.next()
                lw = Pg.dma(lambda e, sw=sw, sc0=sc0, wdt=wdt: e.dma_start(out=WV[sw][:, :, 0:wdt], in_=w[:, :, sc0:sc0 + wdt]),
                            ("wv", sw), deps=wdeps)
                for s in range(NS):
                    b, bdeps = banks.next()
                    last = None
                    for kc in range(KC):
                        d_ = (bdeps + [lw]) if kc == 0 else []
                        last = Pg.op("pe", lambda e, b=b, sw=sw, kc=kc, s=s, wdt=wdt: e.matmul(
                            bank(b)[:, 0:wdt], XT[:, kc, s * P:(s + 1) * P], WV[sw][:, kc, 0:wdt],
                            start=(kc == 0), stop=(kc == KC - 1)), deps=d_)
                    vs, vdeps = vsr.next()
                    eng = ("act", "dve")[evac_i[0] % 2]
                    evac_i[0] += 1
                    if eng == "act":
                        ev = Pg.op("act", lambda e, vs=vs, b=b, wdt=wdt: e.copy(out=VS[vs][:, 0:wdt], in_=bank(b)[:, 0:wdt]), deps=[last] + vdeps)
                    else:
                        ev = Pg.op("dve", lambda e, vs=vs, b=b, wdt=wdt: e.tensor_copy(out=VS[vs][:, 0:wdt], in_=bank(b)[:, 0:wdt]), deps=[last] + vdeps)
                    banks.read(b, ev)
                    r0_ = t0 + s * P
                    stt = Pg.dma(lambda e, vs=vs, r0_=r0_, dc0=dc0, wdt=wdt: e.dma_start(
                        out=Vd[r0_:r0_ + P, dc0:dc0 + wdt], in_=VS[vs][:, 0:wdt]), ("st_v", vs), deps=[ev], eng="act")
                    vsr.read(vs, stt)
                wvr.read(sw, last)
                pc[0] += 1
                pump_to(f0 + (f1 - f0) * pc[0] / (npc * NT))
            xt_readers = [last]
        Pg.barrier()

    def na_phase(li, f0=0.5, f1=0.92):
        A.reset()
        j = li // 2
        QH = [A.view(BF16, [TOK]) for _ in range(2)]
        KH = [A.view(BF16, [TOK]) for _ in range(2)]
        VH = [A.view(BF16, [NBLK, P]) for _ in range(2)]
        NB = [A.view(F32, [2, 7 * P]) for _ in range(2)]
        NAMs = A.view(F32, [32 * 6 * 2])
        DP = 2
        NPIPE = DP + 2
        SBF = [A.view(F32, [6 * P]) for _ in range(NPIPE)]
        PT = [A.view(BF16, [6, P]) for _ in range(NPIPE)]
        RC = [A.view(F32, [P]) for _ in range(2)]
        OTH = [A.view(BF16, [TOK]) for _ in range(2)]
        assert A.off <= CVB, A.off
        hr, sbr, ptr_, rcr, otr = Ring(2), Ring(NPIPE), Ring(NPIPE), Ring(2), Ring(2)
        lnam = Pg.dma(lambda e: e.dma_start(out=NAMs, in_=nam), "nam")
        Vv = Vd.rearrange("(b p) c -> p b c", p=P)
        heads = {}

        def head_setup(h):
            hs, hdeps = hr.next()
            Pg.dma(lambda e: e.dma_start(out=QH[hs], in_=QT[h]), ("nah", hs), deps=hdeps)
            Pg.dma(lambda e: e.dma_start(out=KH[hs], in_=KTd[h]), ("nah", hs))
            Pg.dma(lambda e: e.dma_start(out=NB[hs], in_=nab[j, h].rearrange("p (a b) -> p a b", a=2)), ("nah", hs))
            lh = None
            for b0 in range(0, NBLK, 8):
                lh = Pg.dma(lambda e, b0=b0: e.dma_start(out=VH[hs][:, b0:b0 + 8, :], in_=Vv[:, b0:b0 + 8, h * P:(h + 1) * P]),
                            ("nah", hs))
            os_, odeps = otr.next()
            return dict(hs=hs, lh=lh, os_=os_, odeps=odeps)

        def stage_a(h, i):
            hc = heads[h]
            hs = hc["hs"]
            blocks = NA_TILES[i]
            n = len(blocks)
            na_ = (n + 1) // 2
            groups = [list(range(0, na_)), list(range(na_, n))]
            sb, sbdeps = sbr.next()
            pt, ptdeps = ptr_.next()
            exps = []
            interior = na_interior(i)
            for gi, ks in enumerate(groups):
                b, bdeps = banks.next()
                last = None
                for kk, k in enumerate(ks):
                    jb = blocks[k]
                    d_ = (bdeps + ([hc["lh"]] if i == 0 else [])) if kk == 0 else []
                    last = Pg.op("pe", lambda e, b=b, kk=kk, jb=jb: e.matmul(
                        bank(b)[:, kk * P:(kk + 1) * P], KH[hs][:, jb * P:(jb + 1) * P], QH[hs][:, i * P:(i + 1) * P],
                        start=True, stop=True), deps=d_)
                dl0 = blocks[ks[0]] - i + 3
                w_ = len(ks) * P
                k0 = ks[0]
                var = 1 if interior else 0
                ba = Pg.op("dve", lambda e, b=b, k0=k0, w_=w_, dl0=dl0, var=var: e.scalar_tensor_tensor(
                    out=SBF[sb][:, k0 * P:k0 * P + w_], in0=bank(b)[:, 0:w_], scalar=SCALE,
                    in1=NB[hs][:, var, dl0 * P:dl0 * P + w_], op0=ALU.mult, op1=ALU.add),
                    deps=[last] + (sbdeps if gi == 0 else []))
                banks.read(b, ba)
                if interior:
                    xd = [ba] + (ptdeps if not exps else []) + ([lnam] if (h == 0 and i == 0) else [])
                    exps.append(Pg.op("act", lambda e, k0=k0, w_=w_, nk=len(ks): e.activation(
                        out=PT[pt][:, k0:k0 + nk, :].rearrange("p a b -> p (a b)"), in_=SBF[sb][:, k0 * P:k0 * P + w_], func=AF.Exp),
                        deps=xd))
                    continue
                for k in ks:
                    xd = [ba] + (ptdeps if not exps else []) + ([lnam] if (h == 0 and i == 0) else [])
                    if na_full_valid(i, k):
                        exps.append(Pg.op("act", lambda e, k=k: e.activation(
                            out=PT[pt][:, k, :], in_=SBF[sb][:, k * P:(k + 1) * P], func=AF.Exp), deps=xd))
                    else:
                        for qrl in (0, 1):
                            col = (i * 6 + k) * 2 + qrl
                            exps.append(Pg.op("act", lambda e, k=k, qrl=qrl, col=col: e.activation(
                                out=PT[pt][:, k, qrl * 64:(qrl + 1) * 64],
                                in_=SBF[sb][:, k * P + qrl * 64:k * P + (qrl + 1) * 64],
                                func=AF.Exp, bias=NAMs[:, col:col + 1], scale=1.0), deps=xd))
            sbr.read(sb, exps[-1])
            return dict(pt=pt, ex=exps[-1], blocks=blocks)

        def stage_b(h, i, a):
            hc = heads[h]
            hs, os_ = hc["hs"], hc["os_"]
            pt, blocks = a["pt"], a["blocks"]
            n = len(blocks)
            b, bdeps = banks.next()
            last = None
            for k in range(n):
                jb = blocks[k]
                last = Pg.op("pe", lambda e, k=k, jb=jb: e.matmul(
                    bank(b)[:, 0:P], VH[hs][:, jb, :], PT[pt][:, k, :], start=(k == 0), stop=(k == n - 1)),
                    deps=(bdeps + [a["ex"]]) if k == 0 else [])
            for k in range(n):
                last = Pg.op("pe", lambda e, k=k: e.matmul(
                    bank(b)[:, P:2 * P], ones, PT[pt][:, k, :], start=(k == 0), stop=(k == n - 1)))
            ptr_.read(pt, last)
            rc, rcdeps = rcr.next()
            r1_ = Pg.op("dve", lambda e: e.reciprocal(out=RC[rc], in_=bank(b)[:, P:2 * P]), deps=[last] + rcdeps)
            r2_ = Pg.op("dve", lambda e: e.tensor_tensor(
                out=OTH[os_][:, i * P:(i + 1) * P], in0=bank(b)[:, 0:P], in1=RC[rc], op=ALU.mult),
                deps=[r1_] + (hc["odeps"] if i == 0 else []))
            banks.read(b, r2_)
            rcr.read(rc, r2_)
            if i == 31:
                hr.read(hs, last)
                stt = Pg.dma(lambda e: e.dma_start(out=OT[h * P:(h + 1) * P, :], in_=OTH[os_]), ("st_ot", os_), deps=[r2_], eng="act")
                otr.read(os_, stt)

        items = [(h, i) for h in range(NAH) for i in range(32)]
        queue = []
        for n_, (h, i) in enumerate(items):
            if i == 0:
                heads[h] = head_setup(h)
            queue.append((h, i, stage_a(h, i)))
            if len(queue) > DP:
                stage_b(*queue.pop(0))
        while queue:
            stage_b(*queue.pop(0))
        Pg.barrier()

    def dil_phase(li, f0=0.5, f1=0.92):
        A.reset()
        PAD = 1024
        QD = [A.view(BF16, [TOK]) for _ in range(2)]
        KD = [A.view(BF16, [PAD + TOK + PAD]) for _ in range(2)]
        VMAX = max(d * (TOK // (P * d) + 1) for _, d in DIL)
        VD = [A.view(BF16, [VMAX, P]) for _ in range(2)]
        DB_ = [A.view(F32, [2 * P]) for _ in range(2)]
        DP = 2
        NPIPE = DP + 2
        SBF = [A.view(F32, [2 * P]) for _ in range(NPIPE)]
        PT = [A.view(BF16, [2, P]) for _ in range(NPIPE)]
        UD = A.view(F32, [2, TOK])
        OTD = [A.view(BF16, [TOK]) for _ in range(2)]
        assert A.off <= CVB, A.off
        hr, sbr, ptr_, otr = Ring(2), Ring(NPIPE), Ring(NPIPE), Ring(2)
        zs = []
        for s_ in range(2):
            zs.append(Pg.op("pool", lambda e, s_=s_: e.memset(KD[s_], 0.0)))
            zs.append(Pg.op("pool", lambda e, s_=s_: e.memset(VD[s_], 0.0)))
        st_ = {"ud_readers": [], "ud_last": None, "cnt": 0}
        ctxs = {}

        def group_setup(h, g):
            d = DIL[g][1]
            idx = g * DILH + h
            nblk = TOK // d // P
            hs, hdeps = hr.next()
            zdeps = zs if st_["cnt"] < 2 else []
            st_["cnt"] += 1
            Pg.dma(lambda e: e.dma_start(out=QD[hs], in_=QT[idx]), ("dlh", hs), deps=hdeps + zdeps)
            Pg.dma(lambda e: e.dma_start(out=KD[hs][:, PAD:PAD + TOK], in_=KTd[idx]), ("dlh", hs))
            Pg.dma(lambda e: e.dma_start(out=DB_[hs], in_=dbt[idx]), ("dlh", hs))
            c0 = g * DW + h * P
            Vg = Vd[:, c0:c0 + P]
            VDv = VD[hs][:, 0:d * (nblk + 1), :].rearrange("p (r m) c -> p r m c", r=d)
            Pg.dma(lambda e: e.dma_start(
                out=VDv[64:128, :, 0, :], in_=Vg[0:64 * d, :].rearrange("(p r) c -> p r c", r=d)), ("dlh", hs))
            Pg.dma(lambda e: e.dma_start(
                out=VDv[0:64, :, nblk, :], in_=Vg[TOK - 64 * d:TOK, :].rearrange("(p r) c -> p r c", r=d)), ("dlh", hs))
            lh = None
            full = Vg[64 * d:64 * d + (nblk - 1) * P * d, :].rearrange("(m p r) c -> p r m c", p=P, r=d)
            for r in range(d):
                lh = Pg.dma(lambda e, r=r: e.dma_start(out=VDv[:, r, 1:nblk, :], in_=full[:, r, :, :]), ("dlh", hs))
            return dict(hs=hs, lh=lh, VDv=VDv, d=d, nblk=nblk)

        def stage_a(h, g, r, b_):
            c = ctxs[(h, g)]
            hs, d, nblk = c["hs"], c["d"], c["nblk"]
            qs0 = r + d * P * b_
            qsl = slice(qs0, qs0 + (P - 1) * d + 1, d)
            ka0 = PAD + r + 64 * d * (2 * b_ - 1)
            kb0 = PAD + r + 64 * d * (2 * b_ + 1)
            ksa = slice(ka0, ka0 + (P - 1) * d + 1, d)
            ksb = slice(kb0, kb0 + (P - 1) * d + 1, d)
            b, bdeps = banks.next()
            Pg.op("pe", lambda e: e.matmul(bank(b)[:, 0:P], KD[hs][:, ksa], QD[hs][:, qsl], start=True, stop=True),
                  deps=bdeps + ([c["lh"]] if (r == 0 and b_ == 0) else []))
            last = Pg.op("pe", lambda e: e.matmul(bank(b)[:, P:2 * P], KD[hs][:, ksb], QD[hs][:, qsl], start=True, stop=True))
            sb, sbdeps = sbr.next()
            ba = Pg.op("dve", lambda e: e.scalar_tensor_tensor(
                out=SBF[sb], in0=bank(b)[:, 0:2 * P], scalar=SCALE, in1=DB_[hs], op0=ALU.mult, op1=ALU.add),
                deps=[last] + sbdeps)
            banks.read(b, ba)
            cola = 1 if b_ == 0 else (3 if b_ == nblk // 2 else 0)
            colb = 2 if b_ == nblk - 1 else (4 if b_ == nblk // 2 - 1 else 0)
            pt, ptdeps = ptr_.next()
            if cola == 0 and colb == 0:
                ex = Pg.op("act", lambda e: e.activation(
                    out=PT[pt].rearrange("p a b -> p (a b)"), in_=SBF[sb], func=AF.Exp), deps=[ba] + ptdeps)
            else:
                Pg.op("act", lambda e: e.activation(
                    out=PT[pt][:, 0, :], in_=SBF[sb][:, 0:P], func=AF.Exp, bias=em_sb[:, cola:cola + 1], scale=1.0),
                    deps=[ba] + ptdeps)
                ex = Pg.op("act", lambda e: e.activation(
                    out=PT[pt][:, 1, :], in_=SBF[sb][:, P:2 * P], func=AF.Exp, bias=em_sb[:, colb:colb + 1], scale=1.0))
            sbr.read(sb, ex)
            return dict(pt=pt, ex=ex, qsl=qsl)

        def stage_b(h, g, r, b_, a):
            c = ctxs[(h, g)]
            hs, d, nblk, VDv = c["hs"], c["d"], c["nblk"], c["VDv"]
            pt, qsl = a["pt"], a["qsl"]
            b2, b2deps = banks.next()
            Pg.op("pe", lambda e: e.matmul(bank(b2)[:, 0:P], VDv[:, r, b_, :], PT[pt][:, 0, :], start=True, stop=False),
                  deps=b2deps + [a["ex"]])
            Pg.op("pe", lambda e: e.matmul(bank(b2)[:, 0:P], VDv[:, r, b_ + 1, :], PT[pt][:, 1, :], start=False, stop=True))
            Pg.op("pe", lambda e: e.matmul(bank(b2)[:, P:2 * P], ones, PT[pt][:, 0, :], start=True, stop=False))
            lastpe = Pg.op("pe", lambda e: e.matmul(bank(b2)[:, P:2 * P], ones, PT[pt][:, 1, :], start=False, stop=True))
            ptr_.read(pt, lastpe)
            src2 = bank(b2)[:, 0:2 * P].rearrange("p (a b) -> p a b", a=2)
            dst2 = UD[:, :, qsl]
            if g == 0:
                ac = Pg.op("act", lambda e: e.copy(out=dst2, in_=src2), deps=[lastpe] + st_["ud_readers"])
                st_["ud_readers"] = []
            else:
                ac = Pg.op("dve", lambda e: e.tensor_tensor(out=dst2, in0=src2, in1=dst2, op=ALU.add),
                           deps=[lastpe, st_["ud_prev_group"]])
            banks.read(b2, ac)
            if r == d - 1 and b_ == nblk - 1:
                hr.read(hs, lastpe)
                st_["ud_prev_group"] = ac
                if g == 2:
                    os_, odeps = otr.next()
                    r1_ = Pg.op("dve", lambda e: e.reciprocal(out=UD[:, 1, :], in_=UD[:, 1, :]), deps=[ac])
                    r2_ = Pg.op("dve", lambda e: e.tensor_tensor(out=OTD[os_], in0=UD[:, 0, :], in1=UD[:, 1, :], op=ALU.mult),
                                deps=[r1_] + odeps)
                    st_["ud_readers"] = [r2_]
                    stt = Pg.dma(lambda e: e.dma_start(out=OT[h * P:(h + 1) * P, :], in_=OTD[os_]), ("st_ot", os_), deps=[r2_], eng="act")
                    otr.read(os_, stt)

        items = []
        for h in range(DILH):
            for g, (_, d) in enumerate(DIL):
                for r in range(d):
                    for b_ in range(TOK // d // P):
                        items.append((h, g, r, b_))
        queue = []
        for n_, (h, g, r, b_) in enumerate(items):
            if r == 0 and b_ == 0:
                ctxs[(h, g)] = group_setup(h, g)
            queue.append((h, g, r, b_, stage_a(h, g, r, b_)))
            if len(queue) > DP:
                stage_b(*queue.pop(0))
        while queue:
            stage_b(*queue.pop(0))
        Pg.barrier()

    def out_phase(li):
        A.reset()
        mt, j = li % 2, li // 2
        nC = NAH if mt == 0 else DILH
        wo = (wbf["na_w_o"] if mt == 0 else wbf["dil_w_o"])[j].rearrange("(c p) d -> p c d", p=P)
        OTS = [A.view(BF16, [nC, TT]) for _ in range(2)]
        WO = [A.view(BF16, [nC, DB]) for _ in range(2)]
        YACCS = [A.view(F32, [NS, D]) for _ in range(2)]
        yring = Ring(2)
        XO = [A.view(F32, [D]) for _ in range(1)]
        LNG = A.view(F32, [D])
        LNB = A.view(F32, [D])
        ST = [A.view(F32, [24]) for _ in range(2)]
        MV = [A.view(F32, [2]) for _ in range(2)]
        SD = [A.view(F32, [1]) for _ in range(2)]
        RS = [A.view(F32, [1]) for _ in range(2)]
        otr, wor, xor_, smr = Ring(2), Ring(2), Ring(1), Ring(2)
        assert A.off <= CVB, A.off
        lnrec = load_ln(li * 3 + 1, LNG, LNB, [])
        OTv = OT.rearrange("(c p) t -> p c t", p=P)
        yacc_readers = []
        for t in range(NT):
            t0 = t * TT
            ys, ydeps = yring.next()
            YACC = YACCS[ys]
            lds = []
            for s in range(NS):
                lds.append(Pg.dma(lambda e, s=s, t0=t0, YACC=YACC: e.dma_start(out=YACC[:, s, :], in_=y[t0 + s * P:t0 + (s + 1) * P, :]),
                                  ("yacc", ys), deps=ydeps if s == 0 else ()))
            scl = []
            for s in range(NS):
                scl.append(Pg.op("act", lambda e, s=s, YACC=YACC: e.mul(out=YACC[:, s, :], in_=YACC[:, s, :], mul=ALPHA), deps=[lds[-1]]))
            os_, odeps = otr.next()
            lo = Pg.dma(lambda e, os_=os_, t0=t0: e.dma_start(out=OTS[os_][:, 0:nC, :], in_=OTv[:, 0:nC, t0:t0 + TT]), ("ots", os_), deps=odeps)
            yacc_ready = [[] for _ in range(NS)]
            last = None
            for db in range(NDB):
                sw, wdeps = wor.next()
                lw = Pg.dma(lambda e, sw=sw, db=db: e.dma_start(out=WO[sw], in_=wo[:, :, db * DB:(db + 1) * DB]), ("wo", sw), deps=wdeps)
                for s in range(NS):
                    b, bdeps = banks.next()
                    for c in range(nC):
                        last = Pg.op("pe", lambda e, b=b, c=c, s=s, os_=os_, sw=sw: e.matmul(
                            bank(b)[:, 0:DB], OTS[os_][:, c, s * P:(s + 1) * P], WO[sw][:, c, :],
                            start=(c == 0), stop=(c == nC - 1)), deps=(bdeps + [lw, lo]) if c == 0 else [])
                    ev = Pg.op("dve", lambda e, b=b, s=s, db=db, YACC=YACC: e.scalar_tensor_tensor(
                        out=YACC[:, s, db * DB:(db + 1) * DB], in0=bank(b)[:, 0:DB], scalar=1.0,
                        in1=YACC[:, s, db * DB:(db + 1) * DB], op0=ALU.mult, op1=ALU.add), deps=[last, scl[s]])
                    banks.read(b, ev)
                    yacc_ready[s] = [ev]
                wor.read(sw, last)
            otr.read(os_, last)
            for s in range(NS):
                yacc_ready[s] = yacc_ready[s] + [lnrec]
            for s in range(NS):
                yring.read(ys, ln_sub(t, s, YACC[:, s, :], yacc_ready[s], XO, xor_, LNG, LNB, ST, MV, SD, RS, smr))
            pump_to(0.92 + 0.08 * (t + 1) / NT)
        if pumps["cur"] is not None:
            pumps["cur"].drain()
        Pg.barrier()

    for li in range(DEPTH):
        ffn_phase(li, 0)
        qkv_phase(li)
        if li % 2 == 0:
            na_phase(li)
        else:
            dil_phase(li)
        out_phase(li)
        ffn_phase(li, 1)

    Pg.finalize()
    Pg.emit(nc)
    st.close()
    return nc


def run_cores(cfg, xs, samples, weights):
    nc = build_program(cfg)
    nabt = make_nab(np.asarray(weights["na_rpb"], np.float32)).reshape(cfg.NNA, cfg.NAH, P, 2 * 7 * P)
    dbtt = make_dbt(np.asarray(weights["rel_bias"], np.float32), cfg.DILH).reshape(3 * cfg.DILH, P, 2 * P)
    cstt = np.concatenate([np.eye(P, dtype=np.float32), np.ones((P, P), np.float32)], axis=1)
    common = {
        "ln_g": np.ascontiguousarray(np.asarray(weights["ln_g"], np.float32).reshape(cfg.DEPTH * 3, cfg.D)),
        "ln_b": np.ascontiguousarray(np.asarray(weights["ln_b"], np.float32).reshape(cfg.DEPTH * 3, cfg.D)),
        "nab": nabt, "dbt": dbtt, "cst": cstt,
    }
    for k in ("ffn_w_gate", "ffn_w_up", "ffn_w_down", "na_w_qkv", "na_w_o", "dil_w_qkv", "dil_w_o"):
        common[k] = np.ascontiguousarray(np.asarray(weights[k], np.float32))
    nams = {False: make_nam(False), True: make_nam(True)}
    ems = {False: make_em(False), True: make_em(True)}
    in_maps = []
    for c in range(len(xs)):
        m = dict(common)
        m["x"] = np.ascontiguousarray(xs[c], np.float32)
        m["nam"] = nams[samples[c]]
        m["em"] = ems[samples[c]]
        in_maps.append(m)
    res = run_bass_kernel_spmd(nc, in_maps, core_ids=list(range(len(xs))))
    return [np.asarray(r["y"]) for r in res.results]


def kernel(x_prompt, x_sample, ln_g, ln_b, ffn_w_gate, ffn_w_up, ffn_w_down, na_w_qkv, na_w_o,
           na_rpb, dil_w_qkv, dil_w_o, rel_bias):
    cfg = Cfg()
    x_prompt = np.asarray(x_prompt, np.float32)
    x_sample = np.asarray(x_sample, np.float32)
    xs, samples = [], []
    for c in range(4):
        xs.append(x_prompt[2 * c:2 * c + 2].reshape(cfg.TOK, cfg.D))
        samples.append(False)
    for c in range(4):
        xs.append(x_sample[c])
        samples.append(True)
    weights = dict(ln_g=ln_g, ln_b=ln_b, ffn_w_gate=ffn_w_gate, ffn_w_up=ffn_w_up, ffn_w_down=ffn_w_down,
                   na_w_qkv=na_w_qkv, na_w_o=na_w_o, na_rpb=na_rpb, dil_w_qkv=dil_w_qkv, dil_w_o=dil_w_o,
                   rel_bias=rel_bias)
    ys = run_cores(cfg, xs, samples, weights)
    y_prompt = np.stack([ys[c].reshape(2, 2048, cfg.D) for c in range(4)]).reshape(8, 2048, cfg.D)
    y_sample = np.stack(ys[4:8])
    return (y_prompt.astype(np.float32), y_sample.astype(np.float32))
```
